# Optimizing a Trainium2 kernel written in Bass

```python
import jax, jax.numpy as jnp
from jax import lax
import numpy as np

D_MODEL = 1024
BATCH = 4
SEQ = 4096
DEPTH = 2

CTX_LEN = 256
GRID_W = 64
HEAD_DIM = 64
ROPE_BASE = 10000.0
BLOCK = 128
A_HEADS = 8
A_KV = 2
WINDOW = 128
B_HEADS = 8
B_KV = 2
LRU_WIDTH = D_MODEL
LRU_BLOCKS = 16
LRU_BLOCK_DIM = LRU_WIDTH // LRU_BLOCKS
CONV_W = 4
LRU_C = 8.0
D_FF = 4 * D_MODEL
N_BRANCH = 3
ALPHA = (2 * DEPTH) ** 0.25
BETA = (8 * DEPTH) ** -0.25
LN_EPS = 1e-5
RMS_EPS = 1e-6
NEG_INF = -1e30
IN_COLS = (A_HEADS + 2 * A_KV + B_HEADS + 2 * B_KV) * HEAD_DIM + 2 * LRU_WIDTH + N_BRANCH * D_MODEL

kernel_name = "hybrid_parallel_gated_dit_block"


def _split_cols(z):
    sizes = (A_HEADS * HEAD_DIM, A_KV * HEAD_DIM, A_KV * HEAD_DIM,
             B_HEADS * HEAD_DIM, B_KV * HEAD_DIM, B_KV * HEAD_DIM,
             LRU_WIDTH, LRU_WIDTH, N_BRANCH * D_MODEL)
    return jnp.split(z, np.cumsum(sizes)[:-1].tolist(), axis=-1)


def _layernorm(x, g, b):
    xf = x.astype(jnp.float32)
    mu = jnp.mean(xf, -1, keepdims=True)
    xc = xf - mu
    var = jnp.mean(xc * xc, -1, keepdims=True)
    return (xc * lax.rsqrt(var + LN_EPS) * g + b).astype(x.dtype)


def _rmsnorm(x, g):
    xf = x.astype(jnp.float32)
    return (xf * lax.rsqrt(jnp.mean(xf * xf, -1, keepdims=True) + RMS_EPS) * g).astype(x.dtype)


def _axial_rope_tables(rows):
    row = jnp.repeat(jnp.arange(rows), GRID_W).astype(jnp.float32)
    col = jnp.tile(jnp.arange(GRID_W), rows).astype(jnp.float32)
    nf = HEAD_DIM // 4
    inv = ROPE_BASE ** (-jnp.arange(nf, dtype=jnp.float32) / nf)
    ang_r = row[:, None] * inv
    ang_c = col[:, None] * inv
    return (jnp.cos(ang_r), jnp.sin(ang_r), jnp.cos(ang_c), jnp.sin(ang_c))


def _rotate(x, cos, sin):
    x1, x2 = jnp.split(x, 2, axis=-1)
    return jnp.concatenate([x1 * cos - x2 * sin, x1 * sin + x2 * cos], axis=-1)


def _axial_rope(x, tabs):
    cr, sr, cc, sc = tabs
    half = HEAD_DIM // 2
    xr = _rotate(x[..., :half], cr[:, None, :], sr[:, None, :])
    xcol = _rotate(x[..., half:], cc[:, None, :], sc[:, None, :])
    return jnp.concatenate([xr, xcol], axis=-1).astype(x.dtype)


def _gqa_attend(q, k, v, sink, mask):
    s = jnp.einsum('bqkgd,bjkd->bkgqj', q, k).astype(jnp.float32) * (HEAD_DIM ** -0.5)
    if mask is not None:
        s = jnp.where(mask, s, NEG_INF)
    m = jnp.max(s, -1, keepdims=True)
    if sink is not None:
        sk = sink.astype(jnp.float32)[None, :, :, None, None]
        m = jnp.maximum(m, sk)
    p = jnp.exp(s - m)
    den = jnp.sum(p, -1, keepdims=True)
    if sink is not None:
        den = den + jnp.exp(sk - m)
    o = jnp.einsum('bkgqj,bjkd->bkgqd', p, v.astype(jnp.float32)) / den
    return o.transpose(0, 3, 1, 2, 4).astype(q.dtype)


def _banded_attention(q, k, v, k_ctx, v_ctx, sink):
    bsz, n = q.shape[:2]
    nb = n // BLOCK
    m = k_ctx.shape[1]
    qb = q.reshape(bsz, nb, BLOCK, *q.shape[2:])

    def band(t):
        tp = jnp.pad(t, ((0, 0), (BLOCK, BLOCK), (0, 0), (0, 0))).reshape(bsz, nb + 2, BLOCK, *t.shape[2:])
        return jnp.concatenate([tp[:, :-2], tp[:, 1:-1], tp[:, 2:]], axis=2)

    kb = jnp.concatenate([band(k), jnp.broadcast_to(k_ctx[:, None], (bsz, nb) + k_ctx.shape[1:])], axis=2)
    vb = jnp.concatenate([band(v), jnp.broadcast_to(v_ctx[:, None], (bsz, nb) + v_ctx.shape[1:])], axis=2)
    blk = jnp.arange(nb)[:, None, None]
    qpos = blk * BLOCK + jnp.arange(BLOCK)[None, :, None]
    kpos = blk * BLOCK - BLOCK + jnp.arange(3 * BLOCK)[None, None, :]
    lat_mask = (jnp.abs(kpos - qpos) <= WINDOW) & (kpos >= 0) & (kpos < n)
    mask = jnp.concatenate([lat_mask, jnp.ones((nb, BLOCK, m), dtype=bool)], axis=-1)
    o = jax.vmap(_gqa_attend, in_axes=(1, 1, 1, None, 0), out_axes=1)(qb, kb, vb, sink, mask)
    return o.reshape(bsz, n, -1)


def _blocked_global_attention(q, k_all, v_all):
    bsz, n = q.shape[:2]
    nb = n // BLOCK
    qb = jnp.moveaxis(q.reshape(bsz, nb, BLOCK, *q.shape[2:]), 1, 0)
    o = lax.map(lambda qq: _gqa_attend(qq, k_all, v_all, None, None), qb)
    return jnp.moveaxis(o, 0, 1).reshape(bsz, n, -1)


def _short_conv(x, w, b):
    out = lax.conv_general_dilated(x, w[:, None, :], window_strides=(1,),
                                   padding=[((CONV_W - 1) // 2, CONV_W // 2)],
                                   dimension_numbers=('NWC', 'WIO', 'NWC'),
                                   feature_group_count=x.shape[-1])
    return out + b


def _rglru_gates(x, w_r, b_r, w_i, b_i, lam):
    bsz, n, _ = x.shape
    xb = x.reshape(bsz, n, LRU_BLOCKS, LRU_BLOCK_DIM)
    r = jax.nn.sigmoid(jnp.einsum('bnhd,hde->bnhe', xb, w_r).reshape(bsz, n, LRU_WIDTH) + b_r)
    i = jax.nn.sigmoid(jnp.einsum('bnhd,hde->bnhe', xb, w_i).reshape(bsz, n, LRU_WIDTH) + b_i)
    log_a = -LRU_C * r * jax.nn.softplus(-lam.astype(jnp.float32))
    a = jnp.exp(log_a)
    u = jnp.sqrt(-jnp.expm1(2.0 * log_a)) * (i * x)
    return a, u


def _linear_scan(a, u, h0):
    u = u.at[:, 0].add(a[:, 0] * h0)

    def combine(left, right):
        return left[0] * right[0], right[0] * left[1] + right[1]

    return lax.associative_scan(combine, (a, u), axis=1)[1]


def _rglru_direction(x_ctx, x_lat, w_r, b_r, w_i, b_i, lam, reverse):
    flip = (lambda t: jnp.flip(t, 1)) if reverse else (lambda t: t)
    a_c, u_c = _rglru_gates(x_ctx, w_r, b_r, w_i, b_i, lam)
    h_c = _linear_scan(flip(a_c), flip(u_c), jnp.zeros_like(x_ctx[:, 0]))
    a_l, u_l = _rglru_gates(x_lat, w_r, b_r, w_i, b_i, lam)
    h_l = _linear_scan(flip(a_l), flip(u_l), h_c[:, -1])
    return flip(h_c), flip(h_l)


def _merge(ya, yb, yc, g, w_br_a, w_br_b, w_br_c, w_out):
    ga, gb, gc = jnp.split(jax.nn.sigmoid(g), N_BRANCH, axis=-1)
    return (ga * (ya @ w_br_a) + gb * (yb @ w_br_b) + gc * (yc @ w_br_c)) @ w_out


def _token_mixer(h_lat, h_ctx, rope, w_in, a_sink, b_q_gain, b_k_gain, c_conv_w, c_conv_b,
                 c_wr, c_br, c_wi, c_bi, c_lam, w_br_a, w_br_b, w_br_c, w_out, with_ctx):
    bsz, n, _ = h_lat.shape
    m = h_ctx.shape[1]
    ga, gb = A_HEADS // A_KV, B_HEADS // B_KV
    qa_l, ka_l, va_l, qb_l, kb_l, vb_l, xr_l, yr_l, g_l = _split_cols(h_lat @ w_in)
    qa_c, ka_c, va_c, qb_c, kb_c, vb_c, xr_c, yr_c, g_c = _split_cols(h_ctx @ w_in)

    sink = a_sink.reshape(A_KV, ga)
    qa_l = _axial_rope(qa_l.reshape(bsz, n, A_HEADS, HEAD_DIM), rope).reshape(bsz, n, A_KV, ga, HEAD_DIM)
    ka_l = _axial_rope(ka_l.reshape(bsz, n, A_KV, HEAD_DIM), rope)
    va_l = va_l.reshape(bsz, n, A_KV, HEAD_DIM)
    ka_c = ka_c.reshape(bsz, m, A_KV, HEAD_DIM)
    va_c = va_c.reshape(bsz, m, A_KV, HEAD_DIM)
    ya_l = _banded_attention(qa_l, ka_l, va_l, ka_c, va_c, sink)

    qb_l = _axial_rope(_rmsnorm(qb_l.reshape(bsz, n, B_HEADS, HEAD_DIM), b_q_gain), rope)
    qb_l = qb_l.reshape(bsz, n, B_KV, gb, HEAD_DIM)
    kb_l = _axial_rope(_rmsnorm(kb_l.reshape(bsz, n, B_KV, HEAD_DIM), b_k_gain), rope)
    kb_c = _rmsnorm(kb_c.reshape(bsz, m, B_KV, HEAD_DIM), b_k_gain)
    vb_c = vb_c.reshape(bsz, m, B_KV, HEAD_DIM)
    kb_all = jnp.concatenate([kb_l, kb_c], axis=1)
    vb_all = jnp.concatenate([vb_l.reshape(bsz, n, B_KV, HEAD_DIM), vb_c], axis=1)
    yb_l = _blocked_global_attention(qb_l, kb_all, vb_all)

    xr_l = _short_conv(xr_l, c_conv_w, c_conv_b).astype(jnp.float32)
    xr_c = _short_conv(xr_c, c_conv_w, c_conv_b).astype(jnp.float32)
    hc_f, hl_f = _rglru_direction(xr_c, xr_l, c_wr[0], c_br[0], c_wi[0], c_bi[0], c_lam[0], False)
    hc_b, hl_b = _rglru_direction(xr_c, xr_l, c_wr[1], c_br[1], c_wi[1], c_bi[1], c_lam[1], True)
    yc_l = (hl_f + hl_b).astype(h_lat.dtype) * jax.nn.gelu(yr_l)

    out_lat = _merge(ya_l, yb_l, yc_l, g_l, w_br_a, w_br_b, w_br_c, w_out)
    if not with_ctx:
        return out_lat, None

    ya_c = _gqa_attend(qa_c.reshape(bsz, m, A_KV, ga, HEAD_DIM), ka_c, va_c, sink, None).reshape(bsz, m, -1)
    qb_c = _rmsnorm(qb_c.reshape(bsz, m, B_HEADS, HEAD_DIM), b_q_gain).reshape(bsz, m, B_KV, gb, HEAD_DIM)
    yb_c = _gqa_attend(qb_c, kb_c, vb_c, None, None).reshape(bsz, m, -1)
    yc_c = (hc_f + hc_b).astype(h_ctx.dtype) * jax.nn.gelu(yr_c)
    out_ctx = _merge(ya_c, yb_c, yc_c, g_c, w_br_a, w_br_b, w_br_c, w_out)
    return out_lat, out_ctx


def _sq_relu_mlp(h, w1, w2):
    return jnp.square(jax.nn.relu(h @ w1)) @ w2


def setup_inputs(seed: int = 0) -> dict:
    key = jax.random.key(seed)
    ks = jax.random.split(key, 32)
    f32 = jnp.float32
    L = DEPTH
    bd = LRU_BLOCK_DIM

    def nrm(k, shape, scale):
        return jax.random.normal(k, shape, f32) * scale

    u = jax.random.uniform(ks[17], (L, 2, LRU_WIDTH), f32, 0.9, 0.999)
    p = u ** (1.0 / LRU_C)
    c_lam = jnp.log(p) - jnp.log1p(-p)
    return {
        "x": nrm(ks[0], (BATCH, SEQ, D_MODEL), 1.0),
        "c": nrm(ks[1], (BATCH, D_MODEL), 1.0),
        "ctx": nrm(ks[2], (BATCH, CTX_LEN, D_MODEL), 1.0),
        "c_ctx": nrm(ks[3], (D_MODEL,), 1.0),
        "w_ada": nrm(ks[4], (L, D_MODEL, 6 * D_MODEL), D_MODEL ** -0.5),
        "b_ada": nrm(ks[5], (L, 6 * D_MODEL), 0.02),
        "w_in": nrm(ks[6], (L, D_MODEL, IN_COLS), D_MODEL ** -0.5),
        "a_sink": nrm(ks[7], (L, A_HEADS), 0.5),
        "b_q_gain": 1.0 + nrm(ks[8], (L, HEAD_DIM), 0.02),
        "b_k_gain": 1.0 + nrm(ks[9], (L, HEAD_DIM), 0.02),
        "c_conv_w": nrm(ks[10], (L, CONV_W, LRU_WIDTH), CONV_W ** -0.5),
        "c_conv_b": nrm(ks[11], (L, LRU_WIDTH), 0.02),
        "c_wr": nrm(ks[12], (L, 2, LRU_BLOCKS, bd, bd), bd ** -0.5),
        "c_br": nrm(ks[13], (L, 2, LRU_WIDTH), 0.02),
        "c_wi": nrm(ks[14], (L, 2, LRU_BLOCKS, bd, bd), bd ** -0.5),
        "c_bi": nrm(ks[15], (L, 2, LRU_WIDTH), 0.02),
        "c_lam": c_lam,
        "w_br_a": nrm(ks[18], (L, A_HEADS * HEAD_DIM, D_MODEL), BETA * (A_HEADS * HEAD_DIM) ** -0.5),
        "w_br_b": nrm(ks[19], (L, B_HEADS * HEAD_DIM, D_MODEL), BETA * (B_HEADS * HEAD_DIM) ** -0.5),
        "w_br_c": nrm(ks[20], (L, LRU_WIDTH, D_MODEL), BETA * LRU_WIDTH ** -0.5),
        "w_out": nrm(ks[21], (L, D_MODEL, D_MODEL), BETA * D_MODEL ** -0.5),
        "ln1_g": 1.0 + nrm(ks[22], (L, D_MODEL), 0.02),
        "ln1_b": nrm(ks[23], (L, D_MODEL), 0.02),
        "w_ff1": nrm(ks[24], (L, D_MODEL, D_FF), D_MODEL ** -0.5),
        "w_ff2": nrm(ks[25], (L, D_FF, D_MODEL), BETA * D_FF ** -0.5),
        "ln2_g": 1.0 + nrm(ks[26], (L, D_MODEL), 0.02),
        "ln2_b": nrm(ks[27], (L, D_MODEL), 0.02),
    }


def reference(x, c, ctx, c_ctx, w_ada, b_ada, w_in, a_sink, b_q_gain, b_k_gain, c_conv_w, c_conv_b,
              c_wr, c_br, c_wi, c_bi, c_lam, w_br_a, w_br_b, w_br_c, w_out, ln1_g, ln1_b,
              w_ff1, w_ff2, ln2_g, ln2_b):
    rows = x.shape[1] // GRID_W
    rope = _axial_rope_tables(rows)
    for l in range(DEPTH):
        with_ctx = l < DEPTH - 1
        mod_lat = (jax.nn.silu(c) @ w_ada[l] + b_ada[l])[:, None, :]
        mod_ctx = (jax.nn.silu(c_ctx) @ w_ada[l] + b_ada[l])[None, None, :]
        sh1, sc1, g1, sh2, sc2, g2 = jnp.split(mod_lat, 6, axis=-1)
        csh1, csc1, cg1, csh2, csc2, cg2 = jnp.split(mod_ctx, 6, axis=-1)

        o_lat, o_ctx = _token_mixer(x * (1.0 + sc1) + sh1, ctx * (1.0 + csc1) + csh1, rope,
                                    w_in[l], a_sink[l], b_q_gain[l], b_k_gain[l], c_conv_w[l], c_conv_b[l],
                                    c_wr[l], c_br[l], c_wi[l], c_bi[l], c_lam[l],
                                    w_br_a[l], w_br_b[l], w_br_c[l], w_out[l], with_ctx)
        x = _layernorm(ALPHA * x + g1 * o_lat, ln1_g[l], ln1_b[l])
        x = _layernorm(ALPHA * x + g2 * _sq_relu_mlp(x * (1.0 + sc2) + sh2, w_ff1[l], w_ff2[l]),
                       ln2_g[l], ln2_b[l])
        if with_ctx:
            ctx = _layernorm(ALPHA * ctx + cg1 * o_ctx, ln1_g[l], ln1_b[l])
            ctx = _layernorm(ALPHA * ctx + cg2 * _sq_relu_mlp(ctx * (1.0 + csc2) + csh2, w_ff1[l], w_ff2[l]),
                             ln2_g[l], ln2_b[l])
    return x
```

```python
import os
import numpy as np
from contextlib import ExitStack
import ml_dtypes
import concourse.bass as bass
import concourse.mybir as mybir
from concourse.bass_utils import run_bass_kernel_spmd

F32 = mybir.dt.float32
BF16 = mybir.dt.bfloat16
AF = mybir.ActivationFunctionType
ALU = mybir.AluOpType

D = 1024
NLAT = 4096
NCTX = 256
T = NLAT + NCTX
L = 2
DFF = 4096
INC = 6656
ALPHA = (2 * L) ** 0.25
LN_EPS = 1e-5
RMS_EPS = 1e-6
NBLK = T // 128
NF = 256


class Buf:
    __slots__ = ("w", "r", "excl")

    def __init__(self, excl=False):
        self.w = None
        self.r = {}
        self.excl = excl


class Stream:
    def __init__(self, name, eng, sem, dsems):
        self.name, self.e, self.sem = name, eng, sem
        self.cnt = 0
        self.seen = {}
        self.dsems = dsems
        self.dcnt = [0] * len(dsems)
        self.dlast = [None] * len(dsems)
        self.dbg = [False] * len(dsems)
        self.drr = 0
        self.last = None


class Sched:
    def __init__(self, nc, es):
        self.nc = nc
        self.st = {}
        for name, eng, nd in (("pe", nc.tensor, 0), ("act", nc.scalar, 0), ("dve", nc.vector, 0),
                              ("pool", nc.gpsimd, 10), ("sp", nc.sync, 14)):
            sem = es.enter_context(nc.semaphore(f"s_{name}"))
            ds = [es.enter_context(nc.semaphore(f"d_{name}{i}")) for i in range(nd)]
            self.st[name] = Stream(name, eng, sem, ds)

    def _wait(self, st, evs):
        best = {}
        for ev in evs:
            if ev is None:
                continue
            sem, val = ev
            if st.seen.get(sem, 0) < val and best.get(sem, 0) < val:
                best[sem] = val
        for sem, val in best.items():
            st.e.wait_ge(sem, val)
            st.seen[sem] = val

    def _deps(self, reads, writes):
        evs = []
        for b in reads:
            evs.append(b.w)
            if b.excl:
                evs.extend(b.r.items())
        for b in writes:
            evs.append(b.w)
            evs.extend(b.r.items())
        return evs

    def _mark(self, ev, reads, writes):
        for b in reads:
            b.r[ev[0]] = ev[1]
        for b in writes:
            b.w = ev
            b.r = {}

    def op(self, sname, fn, reads=(), writes=()):
        st = self.st[sname]
        self._wait(st, self._deps(reads, writes))
        ins = fn(st.e)
        st.cnt += 1
        ev = (st.sem, st.cnt)
        ins.then_inc(st.sem, 1)
        st.last = ev
        self._mark(ev, reads, writes)
        return ev

    def mm(self, fns, reads=(), writes=()):
        st = self.st["pe"]
        self._wait(st, [ev for ev in self._deps(reads, writes) if ev is not None and ev[0] is not st.sem])
        ins = None
        for fn in fns:
            ins = fn(st.e)
        st.cnt += 1
        ev = (st.sem, st.cnt)
        ins.then_inc(st.sem, 1)
        st.last = ev
        self._mark(ev, reads, writes)
        return ev

    def dma(self, sname, out, in_, reads=(), writes=(), bg=False):
        st = self.st[sname]
        evs = self._deps(reads, writes)
        i = st.drr
        st.drr = (i + 1) % len(st.dsems)
        evs.append(st.dlast[i])
        self._wait(st, evs)
        ins = st.e.dma_start(out=out, in_=in_)
        st.dcnt[i] += 16
        ev = (st.dsems[i], st.dcnt[i])
        ins.then_inc(st.dsems[i], 16)
        st.dlast[i] = ev
        st.dbg[i] = bg
        self._mark(ev, reads, writes)
        return ev

    def barrier(self, final=False):
        evs = []
        for s in self.st.values():
            evs.append(s.last)
            evs.extend(ev for ev, bg in zip(s.dlast, s.dbg) if final or not bg)
        for s in self.st.values():
            self._wait(s, evs)


class Tl:
    def __init__(self, ap, nsub=1, excl=False):
        self.ap = ap
        self.b = [Buf(excl) for _ in range(nsub)]

    def __getitem__(self, idx):
        return self.ap[idx]


class Pool:
    def __init__(self, tiles):
        self.tiles = tiles
        self.i = 0

    def get(self):
        t = self.tiles[self.i]
        self.i = (self.i + 1) % len(self.tiles)
        return t


def _consts(rev=False):
    ident = np.eye(128, dtype=np.float32)
    perm = np.zeros((128, 128), np.float32)
    for p in range(128):
        perm[p ^ 16, p] = 1.0
    bones = np.zeros((128, 128), np.float32)
    bones[:64, :64] = 1.0 / 64
    bones[64:, 64:] = 1.0 / 64
    onesm = np.full((128, 128), 1.0 / 1024, np.float32)
    pos = np.arange(NLAT)
    row = (pos // 64).astype(np.float32)
    col = (pos % 64).astype(np.float32)
    inv = (10000.0 ** (-np.arange(16, dtype=np.float32) / 16)).astype(np.float32)
    cos = np.zeros((128, NLAT), np.float32)
    sin = np.zeros((128, NLAT), np.float32)
    for p in range(128):
        d = p % 64
        f = d % 16
        ang = ((row if d < 32 else col) * inv[f]).astype(np.float32)
        cos[p] = np.cos(ang)
        sin[p] = np.sin(ang) * (-1.0 if (d % 32) < 16 else 1.0)
    kj = np.arange(128)[:, None]
    qi = np.arange(128)[None, :]
    mask = np.zeros((128, 2, 128), np.float32)
    mask[:, 0, :] = np.where(kj >= qi, 0.0, -30000.0)
    mask[:, 1, :] = np.where(kj <= qi, 0.0, -30000.0)
    if rev:
        cos = np.ascontiguousarray(cos[:, ::-1])
        sin = np.ascontiguousarray(sin[:, ::-1])
    bf = ml_dtypes.bfloat16
    return {"k_ident": ident, "k_perm": perm.astype(bf), "k_bones": bones.astype(bf), "k_onesm": onesm,
            "k_cos": cos, "k_sin": sin, "k_mask": mask.astype(bf), "k_identb": ident.astype(bf)}


WSPEC = [("w_ada", [L, D, 6 * D]), ("b_ada", [L, 6 * D]), ("w_in", [L, D, INC]), ("a_sink", [L, 8]),
         ("b_q_gain", [L, 64]), ("b_k_gain", [L, 64]), ("c_conv_w", [L, 5, D]), ("c_conv_b", [L, D]),
         ("c_wr", [L, 2, 16, 64, 64]), ("c_br", [L, 2, D]), ("c_wi", [L, 2, 16, 64, 64]), ("c_bi", [L, 2, D]),
         ("c_lam", [L, 2, D]), ("w_br_a", [L, 512, D]), ("w_br_b", [L, 512, D]), ("w_br_c", [L, D, D]),
         ("w_out", [L, D, D]), ("ln1_g", [L, D]), ("ln1_b", [L, D]), ("w_ff1", [L, D, DFF]),
         ("w_ff2", [L, DFF, D]), ("ln2_g", [L, D]), ("ln2_b", [L, D])]


def build(dbg=(), nlayers=L, stop=None):
    nc = bass.Bass("TRN2", target_bir_lowering=False)
    es = ExitStack()
    dbg = set(dbg)

    def dram_in(name, shape, dt=F32):
        return nc.dram_tensor(name, list(shape), dt, kind="ExternalInput").ap()

    def scratch(name, shape, dt):
        kind = "ExternalOutput" if name in dbg else "Internal"
        return nc.dram_tensor(name, list(shape), dt, kind=kind).ap()

    x_in = dram_in("x", [NLAT, D])
    ctx_in = dram_in("ctx", [NCTX, D])
    cvec = dram_in("cvec", [2, D])
    W = {n: dram_in(n, s) for n, s in WSPEC}
    k_ident = dram_in("k_ident", [128, 128])
    k_perm = dram_in("k_perm", [128, 128], BF16)
    k_bones = dram_in("k_bones", [128, 128], BF16)
    k_onesm = dram_in("k_onesm", [128, 128])
    k_cos = dram_in("k_cos", [128, NLAT])
    k_sin = dram_in("k_sin", [128, NLAT])
    k_mask = dram_in("k_mask", [128, 2, 128], BF16)
    k_identb = dram_in("k_identb", [128, 128], BF16)
    out = nc.dram_tensor("out", [NLAT // 2, D], F32, kind="ExternalOutput").ap()

    wb_in = [scratch(f"wb_in{l}", [D, INC], BF16) for l in range(L)]
    wb_a = [scratch(f"wb_a{l}", [512, D], BF16) for l in range(L)]
    wb_b = [scratch(f"wb_b{l}", [512, D], BF16) for l in range(L)]
    wb_c = [scratch(f"wb_c{l}", [D, D], BF16) for l in range(L)]
    wb_o = [scratch(f"wb_o{l}", [D, D], BF16) for l in range(L)]
    wb_f1 = [scratch(f"wb_f1{l}", [D, DFF], BF16) for l in range(L)]
    wb_f2 = [scratch(f"wb_f2{l}", [DFF, D], BF16) for l in range(L)]
    xT = [scratch(f"xT{i}", [D, T], F32) for i in range(2)]
    hT = scratch("hT", [D, T], BF16)
    qAT = scratch("qAT", [512, T], BF16)
    qBT = scratch("qBT", [512, T], BF16)
    kAT = scratch("kAT", [128, T], BF16)
    kBT = scratch("kBT", [128, T], BF16)
    vAB = scratch("vAB", [T, 256], BF16)
    xrT = scratch("xrT", [D, T], F32)
    gyT = scratch("gyT", [D, T], F32)
    ycT = scratch("ycT", [D, T], BF16)
    dmod = scratch("dmod", [128, 96], F32)
    dx1 = scratch("dx1", [D, T], F32)
    dya = scratch("dya", [512, T], BF16)
    dyb = scratch("dyb", [512, T], BF16)
    dmt = scratch("dmt", [D, T], BF16)
    dh2 = scratch("dh2", [D, T], BF16)

    S = Sched(nc, es)

    uid = [0]

    def sb(stack, name, shape, dt, nsub=1):
        uid[0] += 1
        return Tl(stack.enter_context(nc.sbuf_tensor(f"{name}_{uid[0]}", list(shape), dt)), nsub)

    def ps(stack, name, shape, dt=F32, nsub=1):
        uid[0] += 1
        return Tl(stack.enter_context(nc.psum_tensor(f"{name}_{uid[0]}", list(shape), dt)), nsub, excl=True)

    def pool_of(stack, name, shape, dt, n, nsub=1, psum=False):
        mk = ps if psum else sb
        return Pool([mk(stack, f"{name}{i}", shape, dt, nsub) for i in range(n)])

    def kview(ap):
        return ap.rearrange("(k p) t -> p k t", p=128)

    ident = sb(es, "ident", [128, 128], F32)
    perm = sb(es, "perm", [128, 128], BF16)
    bones = sb(es, "bones", [128, 128], BF16)
    onesm = sb(es, "onesm", [128, 128], F32)
    PL = []
    for l_ in range(L):
        PL.append(dict(
            mod=sb(es, "mod", [128, 48, 2], F32),
            vecA=sb(es, "vecA", [128, 80], F32),
            vecB=sb(es, "vecB", [128, 96], F32),
            c12=sb(es, "c12", [128, 2, 16], F32),
            esink=sb(es, "esink", [128, 8], F32),
            gains=sb(es, "gains", [128, 2], F32)))
    S.dma("sp", ident[:, :], k_ident, writes=ident.b)
    S.dma("sp", perm[:, :], k_perm, writes=perm.b)
    S.dma("sp", bones[:, :], k_bones, writes=bones.b)
    S.dma("sp", onesm[:, :], k_onesm, writes=onesm.b)
    identb = sb(es, "identb", [128, 128], BF16)
    S.dma("sp", identb[:, :], k_identb, writes=identb.b)

    WB = {}
    castq = []

    def cast_dram(key, dst, src, rows, step):
        WB[key] = []
        for r0 in range(0, rows, step):
            b_ = Buf()
            WB[key].append(b_)
            castq.append((b_, dst[r0:r0 + step, :], src[r0:r0 + step, :]))

    def pump(k):
        for _ in range(min(k, len(castq))):
            b_, d_, s_ = castq.pop(0)
            S.dma("pool", d_, s_, writes=[b_], bg=False)

    for l in range(nlayers):
        cast_dram(("in", l), wb_in[l], W["w_in"][l], D, 256)
        cast_dram(("a", l), wb_a[l], W["w_br_a"][l], 512, 256)
        cast_dram(("b", l), wb_b[l], W["w_br_b"][l], 512, 256)
        cast_dram(("c", l), wb_c[l], W["w_br_c"][l], D, 256)
        cast_dram(("o", l), wb_o[l], W["w_out"][l], D, 256)
        cast_dram(("f1", l), wb_f1[l], W["w_ff1"][l], D, 256)
        cast_dram(("f2", l), wb_f2[l], W["w_ff2"][l], DFF, 512)
    pump(1000)

    def phase_M(l, ph):
        mod, vecA, vecB, c12, esink, gains = (PL[l][k_] for k_ in ("mod", "vecA", "vecB", "c12", "esink", "gains"))
        if True:
            s_raw = sb(ph, "s_raw", [128, 2, 8], F32)
            s_act = sb(ph, "s_act", [128, 2, 8], F32)
            rowsA = sb(ph, "rowsA", [80, 128], F32)
            rowsB = sb(ph, "rowsB", [96, 128], F32)
            wada = pool_of(ph, "wada", [128, 8, 1024], F32, 2)
            pm = ps(ph, "pm", [128, 512], F32)
            pv = ps(ph, "pv", [128, 512], F32)
            tmpc = sb(ph, "tmpc", [128, 16], F32)
            sk = sb(ph, "sk", [128, 8], F32)
            for j in range(2):
                S.dma("sp", s_raw[:, j, :], cvec[j].rearrange("(p k) -> p k", k=8), writes=s_raw.b)
            S.op("act", lambda e: e.activation(out=s_act[:, :, :], in_=s_raw[:, :, :], func=AF.Silu),
                 reads=s_raw.b, writes=s_act.b)
            S.dma("sp", rowsA[0:48, :], W["b_ada"][l].rearrange("(c p) -> c p", p=128), writes=rowsA.b)
            for i, nm in enumerate(("ln1_g", "ln1_b", "ln2_g", "ln2_b")):
                S.dma("sp", rowsA[48 + 8 * i:56 + 8 * i, :], W[nm][l].rearrange("(c p) -> c p", p=128), writes=rowsA.b)
            S.dma("sp", rowsB[0:40, :], W["c_conv_w"][l].rearrange("j (c p) -> (j c) p", p=128), writes=rowsB.b)
            S.dma("sp", rowsB[40:48, :], W["c_conv_b"][l].rearrange("(c p) -> c p", p=128), writes=rowsB.b)
            for i, nm in enumerate(("c_br", "c_bi", "c_lam")):
                S.dma("sp", rowsB[48 + 16 * i:64 + 16 * i, :], W[nm][l].rearrange("d (c p) -> (d c) p", p=128),
                      writes=rowsB.b)
            S.mm([lambda e: e.transpose(out=pv[:, 0:80], in_=rowsA[:, :], identity=ident[0:80, 0:80])],
                 reads=rowsA.b + ident.b, writes=pv.b)
            S.mm([lambda e: e.transpose(out=pv[:, 128:224], in_=rowsB[:, :], identity=ident[0:96, 0:96])],
                 reads=rowsB.b + ident.b, writes=pv.b)
            S.op("act", lambda e: e.activation(out=vecA[:, :], in_=pv[:, 0:80], func=AF.Copy), reads=pv.b, writes=vecA.b)
            S.op("act", lambda e: e.activation(out=vecB[:, :], in_=pv[:, 128:224], func=AF.Copy), reads=pv.b, writes=vecB.b)
            wav = W["w_ada"][l].rearrange("(p k) n -> p k n", k=8)
            for cc in range(6):
                wa = wada.get()
                S.dma("sp", wa[:, :, :], wav[:, :, cc * 1024:(cc + 1) * 1024], writes=wa.b)
                fns = []
                for oc in range(8):
                    for k in range(8):
                        fns.append(lambda e, oc=oc, k=k, wa=wa: e.matmul(
                            pm[:, (cc * 8 + oc) * 2:(cc * 8 + oc) * 2 + 2], lhsT=wa[:, k, oc * 128:(oc + 1) * 128],
                            rhs=s_act[:, :, k], start=(k == 0), stop=(k == 7)))
                S.mm(fns, reads=wa.b + s_act.b, writes=pm.b)
            for j in range(2):
                S.op("dve", lambda e, j=j: e.tensor_tensor(out=mod[:, :, j], in0=pm[:, 0:96].rearrange("p (c j) -> p c j", j=2)[:, :, j],
                                                           in1=vecA[:, 0:48], op=ALU.add), reads=pm.b + vecA.b, writes=mod.b)
            for g in (1, 4):
                S.op("dve", lambda e, g=g: e.tensor_scalar_add(out=mod[:, g * 8:(g + 1) * 8, :], in0=mod[:, g * 8:(g + 1) * 8, :],
                                                               scalar1=1.0), reads=mod.b, writes=mod.b)
            if "dmod" in dbg:
                S.dma("sp", dmod, mod[:, :, :].rearrange("p c j -> p (c j)"), reads=mod.b)
            S.op("act", lambda e: e.activation(out=tmpc[:, :], in_=vecB[:, 80:96], func=AF.Exp, scale=-1.0),
                 reads=vecB.b, writes=tmpc.b)
            S.op("act", lambda e: e.activation(out=tmpc[:, :], in_=tmpc[:, :], func=AF.Ln, bias=1.0, scale=1.0),
                 reads=tmpc.b, writes=tmpc.b)
            S.op("dve", lambda e: e.tensor_scalar_mul(out=c12[:, 0, :], in0=tmpc[:, :], scalar1=-8.0), reads=tmpc.b, writes=c12.b)
            S.op("dve", lambda e: e.tensor_scalar_mul(out=c12[:, 1, :], in0=tmpc[:, :], scalar1=-16.0), reads=tmpc.b, writes=c12.b)
            S.dma("sp", sk[:, :], W["a_sink"][l].partition_broadcast(128), writes=sk.b)
            S.op("act", lambda e: e.activation(out=esink[:, :], in_=sk[:, :], func=AF.Exp), reads=sk.b, writes=esink.b)
            for i, nm in enumerate(("b_q_gain", "b_k_gain")):
                for h in range(2):
                    S.dma("sp", gains[h * 64:(h + 1) * 64, i:i + 1], W[nm][l].rearrange("(p o) -> p o", o=1), writes=gains.b)

    ph0 = ExitStack()
    for l_ in range(nlayers):
        phase_M(l_, ph0)

    with ExitStack() as ph:
        xin = pool_of(ph, "xin", [128, D], F32, 2)
        xst = pool_of(ph, "xst", [128, 8, 128], F32, 2)
        pst = pool_of(ph, "pst", [128, 2, 512], F32, 2, nsub=2, psum=True)
        xTv = kview(xT[0])
        for blk in range(NBLK):
            src = ctx_in[blk * 128:(blk + 1) * 128, :] if blk < 2 else x_in[(blk - 2) * 128:(blk - 1) * 128, :]
            xi = xin.get()
            S.dma("sp", xi[:, :], src, writes=xi.b)
            pp = pst.get()
            S.mm([(lambda e, k=k: e.transpose(out=pp[:, k // 4, (k % 4) * 128:(k % 4 + 1) * 128],
                                               in_=xi[:, k * 128:(k + 1) * 128], identity=ident[:, :]))
                  for k in range(8)], reads=xi.b + ident.b, writes=pp.b)
            xs = xst.get()
            S.op("act", lambda e: e.activation(out=xs[:, 0:4, :], in_=pp[:, 0, :].rearrange("p (k t) -> p k t", k=4),
                                               func=AF.Copy), reads=[pp.b[0]], writes=xs.b)
            S.op("dve", lambda e: e.tensor_copy(out=xs[:, 4:8, :], in_=pp[:, 1, :].rearrange("p (k t) -> p k t", k=4)),
                 reads=[pp.b[1]], writes=xs.b)
            S.dma("sp", xTv[:, :, blk * 128:(blk + 1) * 128], xs[:, :, :], reads=xs.b)
    S.barrier()
    ph0.close()

    for l in range(nlayers):
        with_ctx = l < L - 1
        last = l == L - 1
        xT_cur, xT_nxt = xT[l % 2], xT[(l + 1) % 2]
        mod, vecA, vecB, c12, esink, gains = (PL[l][k_] for k_ in ("mod", "vecA", "vecB", "c12", "esink", "gains"))


        with ExitStack() as ph:
            W1 = sb(ph, "W1", [128, 8, 3584], BF16)
            cosT = sb(ph, "cosT", [128, NLAT], F32)
            sinT = sb(ph, "sinT", [128, NLAT], F32)
            xt_p = pool_of(ph, "xt", [128, 8, 512], F32, 2)
            ht_p = pool_of(ph, "ht", [128, 8, 512], BF16, 2)
            obf = pool_of(ph, "obf", [128, 512], BF16, 6)
            of32 = pool_of(ph, "of32", [128, 512], F32, 4)
            t32 = pool_of(ph, "t32", [128, 512], F32, 14)
            tb16 = pool_of(ph, "tb16", [128, 512], BF16, 8)
            vst = pool_of(ph, "vst", [128, 4, 256], BF16, 2)
            pmain = pool_of(ph, "pmain", [128, 512], F32, 5, psum=True)
            paux = pool_of(ph, "paux", [128, 512], F32, 3, psum=True)
            wv = kview(wb_in[l])
            for base, dst0 in ((0, 0), (768, 640)):
                for j in range(4):
                    S.dma("sp", W1[:, :, dst0 + j * 128:dst0 + j * 128 + 64], wv[:, :, base + j * 64:base + j * 64 + 64], reads=WB[("in", l)], writes=W1.b)
                    S.dma("sp", W1[:, :, dst0 + j * 128 + 64:dst0 + j * 128 + 128],
                          wv[:, :, base + (4 + j) * 64:base + (4 + j) * 64 + 64], reads=WB[("in", l)], writes=W1.b)
            S.dma("sp", W1[:, :, 512:640], wv[:, :, 512:640], reads=WB[("in", l)], writes=W1.b)
            S.dma("sp", W1[:, :, 1152:1280], wv[:, :, 1280:1408], reads=WB[("in", l)], writes=W1.b)
            S.dma("sp", W1[:, :, 1280:1408], wv[:, :, 640:768], reads=WB[("in", l)], writes=W1.b)
            S.dma("sp", W1[:, :, 1408:1536], wv[:, :, 1408:1536], reads=WB[("in", l)], writes=W1.b)
            S.dma("sp", W1[:, :, 1536:2560], wv[:, :, 1536:2560], reads=WB[("in", l)], writes=W1.b)
            S.dma("sp", W1[:, :, 2560:3584], wv[:, :, 2560:3584], reads=WB[("in", l)], writes=W1.b)
            S.dma("sp", cosT[:, :], k_cos, writes=cosT.b)
            S.dma("sp", sinT[:, :], k_sin, writes=sinT.b)
            xcv, hTv = kview(xT_cur), kview(hT)

            tiles = [(0, NCTX)] + [(NCTX + 512 * i, 512) for i in range(8)]
            if stop == "A0a":
                tiles = []
            if stop == "A0b":
                tiles = tiles[:1]
            if stop == "A0c":
                tiles = tiles[1:2]
            for (t0, n) in tiles:
                pump(4)
                is_ctx = t0 == 0
                mj = 1 if is_ctx else 0
                p0 = t0 - NCTX
                mine = (not last) or is_ctx or (t0 < NCTX + NLAT // 2)
                xt = xt_p.get()
                S.dma("sp", xt[:, :, :n], xcv[:, :, t0:t0 + n], writes=xt.b)
                ht = ht_p.get()
                for k in range(8):
                    S.op("dve", lambda e, k=k: e.tensor_scalar(out=ht[:, k, :n], in0=xt[:, k, :n], scalar1=mod[:, 8 + k, mj:mj + 1],
                                                               scalar2=mod[:, k, mj:mj + 1], op0=ALU.mult, op1=ALU.add),
                         reads=xt.b + mod.b, writes=ht.b)
                if mine:
                    S.dma("pool", hTv[:, :, t0:t0 + n], ht[:, :, :n], reads=ht.b)

                def proj(col):
                    pp = pmain.get()
                    S.mm([(lambda e, k=k: e.matmul(pp[:, :n], lhsT=W1[:, k, col:col + 128], rhs=ht[:, k, :n],
                                                   start=(k == 0), stop=(k == 7))) for k in range(8)],
                         reads=W1.b + ht.b, writes=pp.b)
                    return pp

                jobs = []
                if ((not is_ctx) or with_ctx) and mine:
                    jobs += [dict(col=j * 128, dst=qAT, r0=j * 128, g=None) for j in range(4)]
                jobs.append(dict(col=512, dst=kAT, r0=0, g=None))
                if ((not is_ctx) or with_ctx) and mine:
                    jobs += [dict(col=640 + j * 128, dst=qBT, r0=j * 128, g=0) for j in range(4)]
                jobs.append(dict(col=1152, dst=kBT, r0=0, g=1))
                do_rope = not (is_ctx or os.environ.get("NOROPE"))

                def st1(jb):
                    jb["pp"] = proj(jb["col"])

                def st2(jb):
                    if jb["g"] is None:
                        return
                    pp, sq, pa = jb["pp"], tb16.get(), paux.get()
                    S.op("act", lambda e: e.activation(out=sq[:, :n], in_=pp[:, :n], func=AF.Square), reads=pp.b, writes=sq.b)
                    S.mm([lambda e: e.matmul(pa[:, :n], lhsT=bones[:, :], rhs=sq[:, :n], start=True, stop=True)],
                         reads=bones.b + sq.b, writes=pa.b)
                    jb["ms"] = pa

                def st3(jb):
                    src = jb["pp"]
                    if jb["g"] is not None:
                        pa, rs, qn, gcol = jb["ms"], t32.get(), t32.get(), jb["g"]
                        S.op("dve", lambda e: e.tensor_scalar_add(out=rs[:, :n], in0=pa[:, :n], scalar1=RMS_EPS), reads=pa.b, writes=rs.b)
                        S.op("act", lambda e: e.activation(out=rs[:, :n], in_=rs[:, :n], func=AF.Ln), reads=rs.b, writes=rs.b)
                        S.op("act", lambda e: e.activation(out=rs[:, :n], in_=rs[:, :n], func=AF.Exp, scale=-0.5), reads=rs.b, writes=rs.b)
                        S.op("dve", lambda e: e.scalar_tensor_tensor(out=qn[:, :n], in0=src[:, :n], scalar=gains[:, gcol:gcol + 1],
                                                                     in1=rs[:, :n], op0=ALU.mult, op1=ALU.mult),
                             reads=src.b + gains.b + rs.b, writes=qn.b)
                        src = qn
                    if not do_rope:
                        ob = obf.get()
                        S.op("act", lambda e: e.activation(out=ob[:, :n], in_=src[:, :n], func=AF.Copy), reads=src.b, writes=ob.b)
                        S.dma("pool", jb["dst"][jb["r0"]:jb["r0"] + 128, t0:t0 + n], ob[:, :n], reads=ob.b)
                        return
                    qb, a1 = tb16.get(), t32.get()
                    S.op("act", lambda e: e.activation(out=qb[:, :n], in_=src[:, :n], func=AF.Copy), reads=src.b, writes=qb.b)
                    S.op("dve", lambda e: e.tensor_tensor(out=a1[:, :n], in0=src[:, :n], in1=cosT[:, p0:p0 + n], op=ALU.mult),
                         reads=src.b + cosT.b, writes=a1.b)
                    jb["qb"], jb["a1"] = qb, a1

                def st4(jb):
                    if not do_rope:
                        return
                    qb, pa = jb["qb"], paux.get()
                    S.mm([lambda e: e.matmul(pa[:, :n], lhsT=perm[:, :], rhs=qb[:, :n], start=True, stop=True)],
                         reads=perm.b + qb.b, writes=pa.b)
                    jb["pm"] = pa

                def st5(jb):
                    if not do_rope:
                        return
                    pa, a1, a2, ob = jb["pm"], jb["a1"], t32.get(), obf.get()
                    S.op("dve", lambda e: e.tensor_tensor(out=a2[:, :n], in0=pa[:, :n], in1=sinT[:, p0:p0 + n], op=ALU.mult),
                         reads=pa.b + sinT.b, writes=a2.b)
                    S.op("pool", lambda e: e.tensor_tensor(out=ob[:, :n], in0=a1[:, :n], in1=a2[:, :n], op=ALU.add),
                         reads=a1.b + a2.b, writes=ob.b)
                    S.dma("pool", jb["dst"][jb["r0"]:jb["r0"] + 128, t0:t0 + n], ob[:, :n], reads=ob.b)

                stages = (st1, st2, st3, st4, st5)
                for step in range(len(jobs) + len(stages) - 1):
                    for si, fn_ in enumerate(stages):
                        ji = step - si
                        if 0 <= ji < len(jobs):
                            fn_(jobs[ji])
                vs = vst.get()
                for blk in range(n // 128):
                    pp = pmain.get()
                    S.mm([(lambda e, k=k: e.matmul(pp[:, 0:256], lhsT=ht[:, k, blk * 128:(blk + 1) * 128], rhs=W1[:, k, 1280:1536],
                                                   start=(k == 0), stop=(k == 7))) for k in range(8)],
                         reads=W1.b + ht.b, writes=pp.b)
                    S.op("act", lambda e: e.activation(out=vs[:, blk, :], in_=pp[:, 0:256], func=AF.Copy), reads=pp.b, writes=vs.b)
                S.dma("pool", vAB[t0:t0 + n, :].rearrange("(b p) c -> p b c", p=128), vs[:, 0:n // 128, :], reads=vs.b)
                for c in range(8):
                    pp = proj(1536 + c * 128)
                    o = of32.get()
                    S.op("act", lambda e: e.activation(out=o[:, :n], in_=pp[:, :n], func=AF.Copy), reads=pp.b, writes=o.b)
                    S.dma("pool", xrT[c * 128:(c + 1) * 128, t0:t0 + n], o[:, :n], reads=o.b)
                if ((not is_ctx) or with_ctx) and mine:
                    for c in range(8):
                        pp = proj(2560 + c * 128)
                        a1, a2 = t32.get(), t32.get()
                        S.op("act", lambda e: e.activation(out=a1[:, :n], in_=pp[:, :n], func=AF.Square), reads=pp.b, writes=a1.b)
                        S.op("pool", lambda e: e.tensor_scalar(out=a1[:, :n], in0=a1[:, :n], scalar1=0.044715, scalar2=1.0,
                                                               op0=ALU.mult, op1=ALU.add), reads=a1.b, writes=a1.b)
                        S.op("dve", lambda e: e.tensor_tensor(out=a1[:, :n], in0=pp[:, :n], in1=a1[:, :n], op=ALU.mult),
                             reads=pp.b + a1.b, writes=a1.b)
                        S.op("act", lambda e: e.activation(out=a2[:, :n], in_=a1[:, :n], func=AF.Sigmoid, scale=1.5957691216057308),
                             reads=a1.b, writes=a2.b)
                        o = of32.get()
                        S.op("dve", lambda e: e.tensor_tensor(out=o[:, :n], in0=pp[:, :n], in1=a2[:, :n], op=ALU.mult),
                             reads=pp.b + a2.b, writes=o.b)
                        S.dma("pool", gyT[c * 128:(c + 1) * 128, t0:t0 + n], o[:, :n], reads=o.b)
        S.barrier()
        if stop in (f"A{l}", "A0a", "A0b", "A0c"):
            break

        with ExitStack() as ph:
            Wg = sb(ph, "Wg", [128, 2, 2, 8, 128], BF16)
            xr_p = pool_of(ph, "xr", [128, T], F32, 1)
            gy_p = pool_of(ph, "gy", [128, T], F32, 1)
            xc = sb(ph, "xc", [128, T], F32)
            xcb = sb(ph, "xcb", [128, T], BF16)
            a_t = sb(ph, "a_t", [128, T], F32)
            u_t = sb(ph, "u_t", [128, T], F32)
            hf = sb(ph, "hf", [128, T], F32)
            hb = sb(ph, "hb", [128, T], F32)
            ycb = sb(ph, "ycb", [128, T], BF16)
            r_f = sb(ph, "r_f", [128, T], F32)
            i_f = sb(ph, "i_f", [128, T], F32)
            pg = pool_of(ph, "pg", [128, 512], F32, 6, psum=True)
            S.op("pool", lambda e: e.memset(Wg[:, :, :, :, :], 0.0), writes=Wg.b)
            for d in range(2):
                for gi, nm in enumerate(("c_wr", "c_wi")):
                    for par in range(2):
                        src = W[nm][l, d].rearrange("(c two) di e -> two di c e", two=2)[par]
                        S.dma("pool", Wg[par * 64:(par + 1) * 64, d, gi, :, par * 64:(par + 1) * 64], src, writes=Wg.b)
            ctiles = [(0, NCTX)] + [(NCTX + 512 * i, 512) for i in range(8)]
            for c in range(8):
                pump(3)
                xr = xr_p.get()
                gy = gy_p.get()
                S.dma("sp", xr[:, :], xrT[c * 128:(c + 1) * 128, :], writes=xr.b)
                ylo, yhi = (NCTX, NCTX + NLAT // 2) if last else (0, T)
                S.dma("sp", gy[:, ylo:yhi], gyT[c * 128:(c + 1) * 128, ylo:yhi], writes=gy.b)
                w = [vecB[:, j * 8 + c:j * 8 + c + 1] for j in range(5)]
                cb = vecB[:, 40 + c:41 + c]
                for (s, e_) in ((0, NCTX), (NCTX, T)):
                    S.op("act", lambda e, s=s, e_=e_: e.activation(out=xc[:, s:e_], in_=xr[:, s:e_], func=AF.Identity,
                                                                   scale=w[2], bias=cb), reads=xr.b + vecB.b, writes=xc.b)
                    for (j, so, do, ln) in ((0, s, s + 2, e_ - s - 2), (1, s, s + 1, e_ - s - 1), (3, s + 1, s, e_ - s - 1),
                                            (4, s + 2, s, e_ - s - 2)):
                        S.op("dve", lambda e, j=j, so=so, do=do, ln=ln: e.scalar_tensor_tensor(
                            out=xc[:, do:do + ln], in0=xr[:, so:so + ln], scalar=w[j], in1=xc[:, do:do + ln],
                            op0=ALU.mult, op1=ALU.add), reads=xr.b + xc.b + vecB.b, writes=xc.b)
                S.op("act", lambda e: e.activation(out=xcb[:, :], in_=xc[:, :], func=AF.Copy), reads=xc.b, writes=xcb.b)
                for d in range(2):
                    br = vecB[:, 48 + d * 8 + c:49 + d * 8 + c]
                    bi = vecB[:, 64 + d * 8 + c:65 + d * 8 + c]
                    c1 = c12[:, 0, d * 8 + c:d * 8 + c + 1]
                    c2 = c12[:, 1, d * 8 + c:d * 8 + c + 1]
                    TE = (NCTX + NLAT // 2) if (last and d == 0) else T
                    for (t0, n) in ctiles:
                        if t0 >= TE:
                            continue
                        pr, pi = pg.get(), pg.get()
                        S.mm([lambda e: e.matmul(pr[:, :n], lhsT=Wg[:, d, 0, c, :], rhs=xcb[:, t0:t0 + n], start=True, stop=True)],
                             reads=Wg.b + xcb.b, writes=pr.b)
                        S.mm([lambda e: e.matmul(pi[:, :n], lhsT=Wg[:, d, 1, c, :], rhs=xcb[:, t0:t0 + n], start=True, stop=True)],
                             reads=Wg.b + xcb.b, writes=pi.b)
                        S.op("act", lambda e: e.activation(out=r_f[:, t0:t0 + n], in_=pr[:, :n], func=AF.Sigmoid, bias=br),
                             reads=pr.b + vecB.b, writes=r_f.b)
                        S.op("act", lambda e: e.activation(out=i_f[:, t0:t0 + n], in_=pi[:, :n], func=AF.Sigmoid, bias=bi),
                             reads=pi.b + vecB.b, writes=i_f.b)
                    S.op("dve", lambda e: e.tensor_tensor(out=i_f[:, :TE], in0=i_f[:, :TE], in1=xc[:, :TE], op=ALU.mult),
                         reads=i_f.b + xc.b, writes=i_f.b)
                    S.op("act", lambda e: e.activation(out=a_t[:, :TE], in_=r_f[:, :TE], func=AF.Exp, scale=c1),
                         reads=r_f.b + c12.b, writes=a_t.b)
                    S.op("act", lambda e: e.activation(out=r_f[:, :TE], in_=r_f[:, :TE], func=AF.Exp, scale=c2),
                         reads=r_f.b + c12.b, writes=r_f.b)
                    S.op("act", lambda e: e.activation(out=r_f[:, :TE], in_=r_f[:, :TE], func=AF.Sqrt, scale=-1.0, bias=1.0),
                         reads=r_f.b, writes=r_f.b)
                    S.op("dve", lambda e: e.tensor_tensor(out=u_t[:, :TE], in0=r_f[:, :TE], in1=i_f[:, :TE], op=ALU.mult),
                         reads=r_f.b + i_f.b, writes=u_t.b)
                    if d == 0:
                        S.op("dve", lambda e: e.tensor_tensor_scan(out=hf[:, :TE], data0=a_t[:, :TE], data1=u_t[:, :TE], initial=0.0,
                                                                   op0=ALU.mult, op1=ALU.add), reads=a_t.b + u_t.b, writes=hf.b)
                    else:
                        S.op("dve", lambda e: e.tensor_tensor_scan(out=hb[:, 0:NCTX][:, ::-1], data0=a_t[:, 0:NCTX][:, ::-1],
                                                                   data1=u_t[:, 0:NCTX][:, ::-1], initial=0.0,
                                                                   op0=ALU.mult, op1=ALU.add), reads=a_t.b + u_t.b, writes=hb.b)
                        S.op("dve", lambda e: e.tensor_tensor_scan(out=hb[:, NCTX:T][:, ::-1], data0=a_t[:, NCTX:T][:, ::-1],
                                                                   data1=u_t[:, NCTX:T][:, ::-1], initial=hb[:, 0:1],
                                                                   op0=ALU.mult, op1=ALU.add), reads=a_t.b + u_t.b + hb.b, writes=hb.b)
                S.op("pool", lambda e: e.tensor_tensor(out=hf[:, ylo:yhi], in0=hf[:, ylo:yhi], in1=hb[:, ylo:yhi], op=ALU.add),
                     reads=hf.b + hb.b, writes=hf.b)
                S.op("dve", lambda e: e.tensor_tensor(out=ycb[:, ylo:yhi], in0=hf[:, ylo:yhi], in1=gy[:, ylo:yhi], op=ALU.mult),
                     reads=hf.b + gy.b, writes=ycb.b)
                S.dma("pool", ycT[c * 128:(c + 1) * 128, ylo:yhi], ycb[:, ylo:yhi], reads=ycb.b)
        S.barrier()
        if stop == f"C{l}":
            break

        n = NF
        nft = NLAT // n // 2 if last else NLAT // n
        ftiles = ([(0, True)] if with_ctx else []) + [(NCTX + n * i, False) for i in range(nft)]
        xcv, xnv, hTv, ycv = kview(xT_cur), kview(xT_nxt), kview(hT), kview(ycT)
        qAv, qBv, yav, ybv = kview(qAT), kview(qBT), kview(dya), kview(dyb)
        x1v, h2v = kview(dx1), kview(dh2)
        with ExitStack() as ph:
            kA = sb(ph, "kA", [128, T], BF16)
            kB = sb(ph, "kB", [128, T], BF16)
            Vg = sb(ph, "Vg", [128, NBLK, 4, 128], BF16)
            mask = sb(ph, "mask", [128, 2, 128], BF16)
            qA_p = pool_of(ph, "qA", [128, 4, 2, n], BF16, 2)
            qB_p = pool_of(ph, "qB", [128, 4, 2, n], BF16, 2)
            pT_p = pool_of(ph, "pT", [128, 2, 2 * n], BF16, 4)
            pA_p = pool_of(ph, "pA", [128, 640], BF16, 4)
            rd_p = pool_of(ph, "rd", [128, n], F32, 3)
            ya_p = pool_of(ph, "yaT", [128, 4, n], BF16, 2, nsub=4)
            yb_p = pool_of(ph, "ybT", [128, 4, n], BF16, 2, nsub=4)
            pS = pool_of(ph, "pS", [128, 2, 512], F32, 2, psum=True)
            pO = pool_of(ph, "pO", [128, 512], F32, 4, psum=True)

            for tq in qA_p.tiles + qB_p.tiles:
                S.op("pool", lambda e: e.memset(tq[:, :, :, :], 0.0), writes=tq.b)
            S.dma("sp", kA[:, :], kAT, writes=kA.b)
            S.dma("sp", kB[:, :], kBT, writes=kB.b)
            S.dma("sp", mask[:, :, :], k_mask, writes=mask.b)
            S.op("pool", lambda e: e.memset(Vg[:, :, :, :], 1.0), writes=Vg.b)
            vsrc = vAB.rearrange("(b p) (m e) -> p b m e", p=128, e=64)
            for m_ in range(4):
                off = 0 if m_ % 2 == 0 else 64
                for b0 in range(0, NBLK, 9):
                    b1 = min(NBLK, b0 + 9)
                    S.dma("sp", Vg[:, b0:b1, m_, off:off + 64], vsrc[:, b0:b1, m_, :], writes=Vg.b)

            for (t0, is_ctx) in ftiles:
                qA, qB = qA_p.get(), qB_p.get()
                yaT, ybT = ya_p.get(), yb_p.get()
                for (qt, qv) in ((qB, qBv), (qA, qAv)):
                    S.dma("sp", qt[0:64, :, 0, :], qv[0:64, :, t0:t0 + n], writes=qt.b)
                    S.dma("sp", qt[64:128, :, 1, :], qv[64:128, :, t0:t0 + n], writes=qt.b)

                kblocks = [0, 1] if is_ctx else list(range(NBLK))
                groups = [kblocks[i:i + 2] for i in range(0, len(kblocks), 2)]
                for j in range(4):
                    po = [pO.get(), pO.get()]
                    qflat = qB[:, j, :, :].rearrange("p a b -> p (a b)")

                    def pv_b(gi, g, pt2):
                        fns = []
                        for i, kb in enumerate(g):
                            first = (gi == 0 and i == 0)
                            lastf = (gi == len(groups) - 1 and i == len(g) - 1)
                            fns.append(lambda e, i=i, kb=kb, first=first, lastf=lastf: e.matmul(
                                po[0][:, :n], lhsT=Vg[:, kb, 2, :], rhs=pt2[:, i, 0:n], start=first, stop=lastf))
                            fns.append(lambda e, i=i, kb=kb, first=first, lastf=lastf: e.matmul(
                                po[1][:, :n], lhsT=Vg[:, kb, 3, :], rhs=pt2[:, i, n:2 * n], start=first, stop=lastf))
                        S.mm(fns, reads=Vg.b + pt2.b, writes=po[0].b + po[1].b)

                    prev = None
                    for gi, g in enumerate(groups):
                        pst_ = pS.get()
                        S.mm([(lambda e, i=i, kb=kb: e.matmul(pst_[:, i, :], lhsT=kB[:, kb * 128:(kb + 1) * 128], rhs=qflat,
                                                             start=True, stop=True)) for i, kb in enumerate(g)],
                             reads=kB.b + qB.b, writes=pst_.b)
                        pt = pT_p.get()
                        S.op("act", lambda e: e.activation(out=pt[:, 0:len(g), :], in_=pst_[:, 0:len(g), :], func=AF.Exp, scale=0.125),
                             reads=pst_.b, writes=pt.b)
                        if prev is not None:
                            pv_b(*prev)
                        prev = (gi, g, pt)
                    pv_b(*prev)
                    for hh in range(2):
                        orow = slice(0, 64) if hh == 0 else slice(64, 128)
                        drow = slice(64, 128) if hh == 0 else slice(0, 64)
                        rd = rd_p.get()
                        S.op("dve", lambda e: e.reciprocal(out=rd[drow, :], in_=po[hh][drow, :n]), reads=po[hh].b, writes=rd.b)
                        S.op("dve", lambda e: e.tensor_tensor(out=ybT[orow, j, :], in0=po[hh][orow, :n], in1=rd[drow, :], op=ALU.mult),
                             reads=po[hh].b + rd.b, writes=[ybT.b[j]])
                S.dma("pool", ybv[:, :, t0:t0 + n], ybT[:, :, :], reads=ybT.b)

                nqb = n // 128
                prevA = None

                def pv_a(po, hh, qb_, kbs, pa, fin):
                    S.mm([(lambda e, s=s, kb=kb: e.matmul(po[:, qb_ * 128:(qb_ + 1) * 128], lhsT=Vg[:, kb, hh, :],
                                                          rhs=pa[:, s * 128:(s + 1) * 128], start=(s == 0), stop=(s == len(kbs) - 1)))
                          for s, (kb, _) in enumerate(kbs)], reads=Vg.b + pa.b, writes=po.b)
                    if fin is not None:
                        j, h = fin
                        orow = slice(0, 64) if hh == 0 else slice(64, 128)
                        drow = slice(64, 128) if hh == 0 else slice(0, 64)
                        rd = rd_p.get()
                        S.op("dve", lambda e: e.tensor_scalar(out=rd[drow, :], in0=po[drow, :n], scalar1=esink[drow, h:h + 1], scalar2=None,
                                                              op0=ALU.add), reads=po.b + esink.b, writes=rd.b)
                        S.op("dve", lambda e: e.reciprocal(out=rd[drow, :], in_=rd[drow, :]), reads=rd.b, writes=rd.b)
                        S.op("dve", lambda e: e.tensor_tensor(out=yaT[orow, j, :], in0=po[orow, :n], in1=rd[drow, :], op=ALU.mult),
                             reads=po.b + rd.b, writes=[yaT.b[j]])

                for j in range(4):
                    for hh in range(2):
                        h = j + 4 * hh
                        po = pO.get()
                        for qb_ in range(nqb):
                            if is_ctx:
                                kbs = [(0, None), (1, None)]
                            else:
                                B = (t0 - NCTX) // 128 + qb_
                                kbs = [(0, None), (1, None)]
                                if B - 1 >= 0:
                                    kbs.append((B - 1 + 2, 0))
                                kbs.append((B + 2, None))
                                if B + 1 < NLAT // 128:
                                    kbs.append((B + 1 + 2, 1))
                            pst_ = pS.get()
                            pflat = pst_[:, :, :].rearrange("p a b -> p (a b)")
                            fns = []
                            for s, (kb, mi) in enumerate(kbs):
                                fns.append(lambda e, s=s, kb=kb, mi=mi: e.matmul(
                                    pflat[:, s * 128:(s + 1) * 128], lhsT=kA[:, kb * 128:(kb + 1) * 128],
                                    rhs=qA[:, j, hh, qb_ * 128:(qb_ + 1) * 128], start=True, stop=(mi is None)))
                                if mi is not None:
                                    fns.append(lambda e, s=s, mi=mi: e.matmul(
                                        pflat[:, s * 128:(s + 1) * 128], lhsT=identb[:, :], rhs=mask[:, mi, :], start=False, stop=True))
                            S.mm(fns, reads=kA.b + qA.b + identb.b + mask.b, writes=pst_.b)
                            pa = pA_p.get()
                            nk = len(kbs) * 128
                            S.op("act", lambda e: e.activation(out=pa[:, 0:nk], in_=pflat[:, 0:nk], func=AF.Exp, scale=0.125),
                                 reads=pst_.b, writes=pa.b)
                            if prevA is not None:
                                pv_a(*prevA)
                            prevA = (po, hh, qb_, kbs, pa, (j, h) if qb_ == nqb - 1 else None)
                pv_a(*prevA)
                S.dma("pool", yav[:, :, t0:t0 + n], yaT[:, :, :], reads=yaT.b)
        S.barrier()
        if stop == f"F1{l}":
            break

        def layer_norm(xres, h2, pO, ln_p, lmean, lvar, mj, gofs, bofs, sc_g, sh_g):
            pmean, pex2 = pO.get(), pO.get()
            for oc in range(8):
                sq = ln_p.get()
                S.op("act", lambda e: e.activation(out=sq[:, :], in_=xres[:, oc, :], func=AF.Square),
                     reads=[xres.b[oc]], writes=sq.b)
                S.mm([lambda e: e.matmul(pmean[:, :n], lhsT=onesm[:, :], rhs=xres[:, oc, :], start=(oc == 0), stop=(oc == 7))],
                     reads=onesm.b + [xres.b[oc]], writes=pmean.b)
                S.mm([lambda e: e.matmul(pex2[:, :n], lhsT=onesm[:, :], rhs=sq[:, :], start=(oc == 0), stop=(oc == 7))],
                     reads=onesm.b + sq.b, writes=pex2.b)
            mean, var = lmean, lvar
            S.op("act", lambda e: e.activation(out=mean[:, :], in_=pmean[:, :n], func=AF.Copy), reads=pmean.b, writes=mean.b)
            S.op("act", lambda e: e.activation(out=var[:, :], in_=pmean[:, :n], func=AF.Square), reads=pmean.b, writes=var.b)
            S.op("dve", lambda e: e.tensor_tensor(out=var[:, :], in0=pex2[:, :n], in1=var[:, :], op=ALU.subtract),
                 reads=pex2.b + var.b, writes=var.b)
            S.op("dve", lambda e: e.tensor_scalar_add(out=var[:, :], in0=var[:, :], scalar1=LN_EPS), reads=var.b, writes=var.b)
            S.op("act", lambda e: e.activation(out=var[:, :], in_=var[:, :], func=AF.Sqrt), reads=var.b, writes=var.b)
            S.op("dve", lambda e: e.reciprocal(out=var[:, :], in_=var[:, :]), reads=var.b, writes=var.b)
            for oc in range(8):
                tt = ln_p.get()
                S.op("pool", lambda e: e.tensor_tensor(out=tt[:, :], in0=xres[:, oc, :], in1=mean[:, :], op=ALU.subtract),
                     reads=[xres.b[oc]] + mean.b, writes=tt.b)
                S.op("pool", lambda e: e.tensor_tensor(out=tt[:, :], in0=tt[:, :], in1=var[:, :], op=ALU.mult),
                     reads=tt.b + var.b, writes=tt.b)
                S.op("act", lambda e: e.activation(out=xres[:, oc, :], in_=tt[:, :], func=AF.Identity,
                                                   scale=vecA[:, gofs + oc:gofs + oc + 1], bias=vecA[:, bofs + oc:bofs + oc + 1]),
                     reads=tt.b + vecA.b, writes=[xres.b[oc]])
                if h2 is not None:
                    S.op("dve", lambda e: e.tensor_scalar(out=h2[:, oc, :], in0=xres[:, oc, :], scalar1=mod[:, sc_g * 8 + oc, mj:mj + 1],
                                                          scalar2=mod[:, sh_g * 8 + oc, mj:mj + 1], op0=ALU.mult, op1=ALU.add),
                         reads=[xres.b[oc]] + mod.b, writes=[h2.b[oc]])

        def resid(xres, oc, pp, gidx, mj):
            S.op("act", lambda e: e.activation(out=xres[:, oc, :], in_=xres[:, oc, :], func=AF.Copy, scale=float(ALPHA)),
                 reads=[xres.b[oc]], writes=[xres.b[oc]])
            S.op("dve", lambda e: e.scalar_tensor_tensor(out=xres[:, oc, :], in0=pp[:, :n], scalar=mod[:, gidx * 8 + oc, mj:mj + 1],
                                                         in1=xres[:, oc, :], op0=ALU.mult, op1=ALU.add),
                 reads=pp.b + [xres.b[oc]] + mod.b, writes=[xres.b[oc]])

        with ExitStack() as ph:
            Wgt = sb(ph, "Wgt", [128, 8, 3072], BF16)
            Wa = sb(ph, "Wa", [128, 4, D], BF16)
            Wb = sb(ph, "Wb", [128, 4, D], BF16)
            Wc = sb(ph, "Wc", [128, 8, D], BF16)
            Wo = sb(ph, "Wo", [128, 8, D], BF16)
            ya_p = pool_of(ph, "ya2", [128, 4, n], BF16, 2)
            yb_p = pool_of(ph, "yb2", [128, 4, n], BF16, 2)
            yc_p = pool_of(ph, "yc2", [128, 8, n], BF16, 2)
            ht_p = pool_of(ph, "ht2", [128, 8, n], BF16, 2)
            xr_p = pool_of(ph, "xres2", [128, 8, n], F32, 2, nsub=8)
            h2_p = pool_of(ph, "h22", [128, 8, n], BF16, 2, nsub=8)
            mT_p = pool_of(ph, "mT", [128, 8, n], BF16, 1, nsub=8)
            sg_p = pool_of(ph, "sg", [128, n], F32, 4)
            mt_p = pool_of(ph, "mt", [128, n], F32, 6)
            ln_p = pool_of(ph, "lnt", [128, n], F32, 6)
            lmean = sb(ph, "lmean", [128, n], F32)
            lvar = sb(ph, "lvar", [128, n], F32)
            pO = pool_of(ph, "pO2", [128, 512], F32, 8, psum=True)
            wiv = kview(wb_in[l])
            for gi in range(3):
                for hlf in range(2):
                    c0 = 3584 + gi * 1024 + hlf * 512
                    S.dma("sp", Wgt[:, :, gi * 1024 + hlf * 512:gi * 1024 + (hlf + 1) * 512], wiv[:, :, c0:c0 + 512], reads=WB[("in", l)], writes=Wgt.b)
            for (wt, wsrc, wk) in ((Wa, wb_a[l], "a"), (Wb, wb_b[l], "b")):
                hv = wsrc.rearrange("(h d) n -> d h n", d=64)
                S.dma("sp", wt[0:64, :, :], hv[:, 0:4, :], reads=WB[(wk, l)], writes=wt.b)
                S.dma("sp", wt[64:128, :, :], hv[:, 4:8, :], reads=WB[(wk, l)], writes=wt.b)
            for hlf in range(2):
                S.dma("sp", Wc[:, :, hlf * 512:(hlf + 1) * 512], kview(wb_c[l])[:, :, hlf * 512:(hlf + 1) * 512], reads=WB[("c", l)], writes=Wc.b)
                S.dma("sp", Wo[:, :, hlf * 512:(hlf + 1) * 512], kview(wb_o[l])[:, :, hlf * 512:(hlf + 1) * 512], reads=WB[("o", l)], writes=Wo.b)
            for (t0, is_ctx) in ftiles:
                mj = 1 if is_ctx else 0
                yaT, ybT, ycs, hts = ya_p.get(), yb_p.get(), yc_p.get(), ht_p.get()
                xres, h2, mT = xr_p.get(), h2_p.get(), mT_p.get()
                S.dma("sp", hts[:, :, :], hTv[:, :, t0:t0 + n], writes=hts.b)
                S.dma("sp", yaT[:, :, :], yav[:, :, t0:t0 + n], writes=yaT.b)
                S.dma("sp", ybT[:, :, :], ybv[:, :, t0:t0 + n], writes=ybT.b)
                S.dma("sp", ycs[:, :, :], ycv[:, :, t0:t0 + n], writes=ycs.b)
                S.dma("sp", xres[:, :, :], xcv[:, :, t0:t0 + n], writes=xres.b)
                for oc in range(8):
                    cs = slice(oc * 128, (oc + 1) * 128)
                    acc = None
                    for gi, (wt, ysrc, nk) in enumerate(((Wa, yaT, 4), (Wb, ybT, 4), (Wc, ycs, 8))):
                        pgate = pO.get()
                        S.mm([(lambda e, k=k: e.matmul(pgate[:, :n], lhsT=Wgt[:, k, gi * 1024 + oc * 128:gi * 1024 + (oc + 1) * 128],
                                                       rhs=hts[:, k, :], start=(k == 0), stop=(k == 7))) for k in range(8)],
                             reads=Wgt.b + hts.b, writes=pgate.b)
                        sg = sg_p.get()
                        S.op("act", lambda e: e.activation(out=sg[:, :], in_=pgate[:, :n], func=AF.Sigmoid), reads=pgate.b, writes=sg.b)
                        pbr = pO.get()
                        S.mm([(lambda e, k=k: e.matmul(pbr[:, :n], lhsT=wt[:, k, cs], rhs=ysrc[:, k, :],
                                                       start=(k == 0), stop=(k == nk - 1))) for k in range(nk)],
                             reads=wt.b + ysrc.b, writes=pbr.b)
                        mt = mt_p.get()
                        S.op("dve", lambda e: e.tensor_tensor(out=mt[:, :], in0=pbr[:, :n], in1=sg[:, :], op=ALU.mult),
                             reads=pbr.b + sg.b, writes=mt.b)
                        if acc is None:
                            acc = mt
                        elif gi == 1:
                            S.op("pool", lambda e: e.tensor_tensor(out=mt[:, :], in0=mt[:, :], in1=acc[:, :], op=ALU.add),
                                 reads=mt.b + acc.b, writes=mt.b)
                            acc = mt
                        else:
                            S.op("dve", lambda e: e.tensor_tensor(out=mT[:, oc, :], in0=mt[:, :], in1=acc[:, :], op=ALU.add),
                                 reads=mt.b + acc.b, writes=[mT.b[oc]])
                for oc in range(8):
                    pp = pO.get()
                    S.mm([(lambda e, k=k: e.matmul(pp[:, :n], lhsT=Wo[:, k, oc * 128:(oc + 1) * 128], rhs=mT[:, k, :],
                                                   start=(k == 0), stop=(k == 7))) for k in range(8)],
                         reads=Wo.b + mT.b, writes=pp.b)
                    resid(xres, oc, pp, 2, mj)
                layer_norm(xres, h2, pO, ln_p, lmean, lvar, mj, 48, 56, 4, 3)
                S.dma("pool", x1v[:, :, t0:t0 + n], xres[:, :, :], reads=xres.b)
                S.dma("pool", h2v[:, :, t0:t0 + n], h2[:, :, :], reads=h2.b)
        S.barrier()
        if stop == f"F2{l}":
            break

        with ExitStack() as ph:
            W1f = sb(ph, "W1f", [128, 8, DFF], BF16)
            W2f = sb(ph, "W2f", [128, 32, D], BF16)
            xr_p = pool_of(ph, "xres3", [128, 8, n], F32, 2, nsub=8)
            h2_p = pool_of(ph, "h23", [128, 8, n], BF16, 2)
            fT = sb(ph, "fT", [128, 32, n], BF16, nsub=32)
            rl_p = pool_of(ph, "rl", [128, n], F32, 4)
            ln_p = pool_of(ph, "lnt3", [128, n], F32, 4)
            lmean = sb(ph, "lmean3", [128, n], F32)
            lvar = sb(ph, "lvar3", [128, n], F32)
            ost_p = pool_of(ph, "ost", [128, D], F32, 2)
            pO = pool_of(ph, "pO3", [128, 512], F32, 6, psum=True)
            pT2 = pool_of(ph, "pT2", [128, 2, 512], F32, 1, psum=True)
            w1v, w2v = kview(wb_f1[l]), kview(wb_f2[l])
            for q4 in range(4):
                S.dma("sp", W1f[:, :, q4 * 1024:(q4 + 1) * 1024], w1v[:, :, q4 * 1024:(q4 + 1) * 1024], reads=WB[("f1", l)], writes=W1f.b)
            for q4 in range(4):
                S.dma("sp", W2f[:, q4 * 8:(q4 + 1) * 8, :], w2v[:, q4 * 8:(q4 + 1) * 8, :], reads=WB[("f2", l)], writes=W2f.b)
            for (t0, is_ctx) in ftiles:
                mj = 1 if is_ctx else 0
                xres, h2 = xr_p.get(), h2_p.get()
                S.dma("sp", h2[:, :, :], h2v[:, :, t0:t0 + n], writes=h2.b)
                S.dma("sp", xres[:, :, :], x1v[:, :, t0:t0 + n], writes=xres.b)
                for fc in range(32):
                    pp = pO.get()
                    S.mm([(lambda e, k=k: e.matmul(pp[:, :n], lhsT=W1f[:, k, fc * 128:(fc + 1) * 128], rhs=h2[:, k, :],
                                                   start=(k == 0), stop=(k == 7))) for k in range(8)],
                         reads=W1f.b + h2.b, writes=pp.b)
                    rl = rl_p.get()
                    S.op("act", lambda e: e.activation(out=rl[:, :], in_=pp[:, :n], func=AF.Relu), reads=pp.b, writes=rl.b)
                    S.op("dve" if fc % 2 == 0 else "pool", lambda e: e.tensor_tensor(out=fT[:, fc, :], in0=rl[:, :], in1=rl[:, :], op=ALU.mult),
                         reads=rl.b, writes=[fT.b[fc]])
                for oc in range(8):
                    pp = pO.get()
                    S.mm([(lambda e, k=k: e.matmul(pp[:, :n], lhsT=W2f[:, k, oc * 128:(oc + 1) * 128], rhs=fT[:, k, :],
                                                   start=(k == 0), stop=(k == 31))) for k in range(32)],
                         reads=W2f.b + fT.b, writes=pp.b)
                    resid(xres, oc, pp, 5, mj)
                layer_norm(xres, None, pO, ln_p, lmean, lvar, mj, 64, 72, 0, 0)
                if not last:
                    S.dma("pool", xnv[:, :, t0:t0 + n], xres[:, :, :], reads=xres.b)
                else:
                    for blk in range(n // 128):
                        ptp = pT2.get()
                        S.mm([(lambda e, k=k: e.transpose(out=ptp[:, k // 4, (k % 4) * 128:(k % 4 + 1) * 128],
                                                          in_=xres[:, k, blk * 128:(blk + 1) * 128], identity=ident[:, :])) for k in range(8)],
                             reads=xres.b + ident.b, writes=ptp.b)
                        ost = ost_p.get()
                        S.op("act", lambda e: e.activation(out=ost[:, 0:512], in_=ptp[:, 0, :], func=AF.Copy), reads=ptp.b, writes=ost.b)
                        S.op("dve", lambda e: e.tensor_copy(out=ost[:, 512:1024], in_=ptp[:, 1, :]), reads=ptp.b, writes=ost.b)
                        r0 = t0 - NCTX + blk * 128
                        S.dma("pool", out[r0:r0 + 128, :], ost[:, :], reads=ost.b)
        S.barrier()
        if stop == f"F{l}":
            break

    pump(1000)
    S.barrier(final=True)
    es.close()
    return nc


def make_in_maps(inputs, ncores=8):
    f = lambda a: np.ascontiguousarray(np.asarray(a, dtype=np.float32))
    shared = {n: f(inputs[n]) for n, _ in WSPEC if n not in ("c_conv_w", "c_wr", "c_br", "c_wi", "c_bi", "c_lam")}
    cw = f(inputs["c_conv_w"])
    z = np.zeros_like(cw[:, :1])
    par = []
    for rev in (False, True):
        d = {}
        d["c_conv_w"] = np.ascontiguousarray(np.concatenate([cw[:, ::-1], z], 1) if rev else np.concatenate([z, cw], 1))
        for n in ("c_wr", "c_br", "c_wi", "c_bi", "c_lam"):
            a = f(inputs[n])
            d[n] = np.ascontiguousarray(a[:, ::-1]) if rev else a
        d.update(_consts(rev))
        par.append(d)
    maps = []
    for core in range(ncores):
        b, rev = (core // 2) % 4, core % 2
        xb, cb = f(inputs["x"][b]), f(inputs["ctx"][b])
        if rev:
            xb, cb = np.ascontiguousarray(xb[::-1]), np.ascontiguousarray(cb[::-1])
        m = {"x": xb, "ctx": cb,
             "cvec": f(np.stack([np.asarray(inputs["c"][b]), np.asarray(inputs["c_ctx"])], 0))}
        m.update(shared)
        m.update(par[rev])
        maps.append(m)
    return maps


def kernel(**inputs):
    nc = build()
    maps = make_in_maps(inputs)
    res = run_bass_kernel_spmd(nc, maps, core_ids=list(range(8)))
    outs = []
    for b in range(4):
        lo = np.asarray(res.results[2 * b]["out"], dtype=np.float32)
        hi = np.asarray(res.results[2 * b + 1]["out"], dtype=np.float32)[::-1]
        outs.append(np.concatenate([lo, hi], 0))
    return np.stack(outs, 0)
```

```python
import os
import numpy as np
from contextlib import ExitStack
import ml_dtypes
import concourse.bass as bass
import concourse.mybir as mybir
from concourse.bass_utils import run_bass_kernel_spmd

F32 = mybir.dt.float32
BF16 = mybir.dt.bfloat16
AF = mybir.ActivationFunctionType
ALU = mybir.AluOpType

D = 1024
NLAT = 4096
NCTX = 256
T = NLAT + NCTX
L = 2
DFF = 4096
INC = 6656
ALPHA = (2 * L) ** 0.25
LN_EPS = 1e-5
RMS_EPS = 1e-6
NBLK = T // 128
NF = 256


class Buf:
    __slots__ = ("w", "r", "excl")

    def __init__(self, excl=False):
        self.w = None
        self.r = {}
        self.excl = excl


class Stream:
    def __init__(self, name, eng, sem, dsems):
        self.name, self.e, self.sem = name, eng, sem
        self.cnt = 0
        self.seen = {}
        self.dsems = dsems
        self.dcnt = [0] * len(dsems)
        self.dlast = [None] * len(dsems)
        self.dbg = [False] * len(dsems)
        self.drr = 0
        self.last = None


class Sched:
    def __init__(self, nc, es):
        self.nc = nc
        self.st = {}
        for name, eng, nd in (("pe", nc.tensor, 0), ("act", nc.scalar, 0), ("dve", nc.vector, 0),
                              ("pool", nc.gpsimd, 10), ("sp", nc.sync, 14)):
            sem = es.enter_context(nc.semaphore(f"s_{name}"))
            ds = [es.enter_context(nc.semaphore(f"d_{name}{i}")) for i in range(nd)]
            self.st[name] = Stream(name, eng, sem, ds)

    def _wait(self, st, evs):
        best = {}
        for ev in evs:
            if ev is None:
                continue
            sem, val = ev
            if st.seen.get(sem, 0) < val and best.get(sem, 0) < val:
                best[sem] = val
        for sem, val in best.items():
            st.e.wait_ge(sem, val)
            st.seen[sem] = val

    def _deps(self, reads, writes):
        evs = []
        for b in reads:
            evs.append(b.w)
            if b.excl:
                evs.extend(b.r.items())
        for b in writes:
            evs.append(b.w)
            evs.extend(b.r.items())
        return evs

    def _mark(self, ev, reads, writes):
        for b in reads:
            b.r[ev[0]] = ev[1]
        for b in writes:
            b.w = ev
            b.r = {}

    def op(self, sname, fn, reads=(), writes=()):
        st = self.st[sname]
        self._wait(st, self._deps(reads, writes))
        ins = fn(st.e)
        st.cnt += 1
        ev = (st.sem, st.cnt)
        ins.then_inc(st.sem, 1)
        st.last = ev
        self._mark(ev, reads, writes)
        return ev

    def mm(self, fns, reads=(), writes=()):
        st = self.st["pe"]
        self._wait(st, [ev for ev in self._deps(reads, writes) if ev is not None and ev[0] is not st.sem])
        ins = None
        for fn in fns:
            ins = fn(st.e)
        st.cnt += 1
        ev = (st.sem, st.cnt)
        ins.then_inc(st.sem, 1)
        st.last = ev
        self._mark(ev, reads, writes)
        return ev

    def dma(self, sname, out, in_, reads=(), writes=(), bg=False):
        st = self.st[sname]
        evs = self._deps(reads, writes)
        i = st.drr
        st.drr = (i + 1) % len(st.dsems)
        evs.append(st.dlast[i])
        self._wait(st, evs)
        ins = st.e.dma_start(out=out, in_=in_)
        st.dcnt[i] += 16
        ev = (st.dsems[i], st.dcnt[i])
        ins.then_inc(st.dsems[i], 16)
        st.dlast[i] = ev
        st.dbg[i] = bg
        self._mark(ev, reads, writes)
        return ev

    def barrier(self, final=False):
        evs = []
        for s in self.st.values():
            evs.append(s.last)
            evs.extend(ev for ev, bg in zip(s.dlast, s.dbg) if final or not bg)
        for s in self.st.values():
            self._wait(s, evs)


class Tl:
    def __init__(self, ap, nsub=1, excl=False):
        self.ap = ap
        self.b = [Buf(excl) for _ in range(nsub)]

    def __getitem__(self, idx):
        return self.ap[idx]


class Pool:
    def __init__(self, tiles):
        self.tiles = tiles
        self.i = 0

    def get(self):
        t = self.tiles[self.i]
        self.i = (self.i + 1) % len(self.tiles)
        return t


def _consts(rev=False):
    ident = np.eye(128, dtype=np.float32)
    perm = np.zeros((128, 128), np.float32)
    for p in range(128):
        perm[p ^ 16, p] = 1.0
    bones = np.zeros((128, 128), np.float32)
    bones[:64, :64] = 1.0 / 64
    bones[64:, 64:] = 1.0 / 64
    onesm = np.full((128, 128), 1.0 / 1024, np.float32)
    pos = np.arange(NLAT)
    row = (pos // 64).astype(np.float32)
    col = (pos % 64).astype(np.float32)
    inv = (10000.0 ** (-np.arange(16, dtype=np.float32) / 16)).astype(np.float32)
    cos = np.zeros((128, NLAT), np.float32)
    sin = np.zeros((128, NLAT), np.float32)
    for p in range(128):
        d = p % 64
        f = d % 16
        ang = ((row if d < 32 else col) * inv[f]).astype(np.float32)
        cos[p] = np.cos(ang)
        sin[p] = np.sin(ang) * (-1.0 if (d % 32) < 16 else 1.0)
    kj = np.arange(128)[:, None]
    qi = np.arange(128)[None, :]
    mask = np.zeros((128, 2, 128), np.float32)
    mask[:, 0, :] = np.where(kj >= qi, 0.0, -30000.0)
    mask[:, 1, :] = np.where(kj <= qi, 0.0, -30000.0)
    if rev:
        cos = np.ascontiguousarray(cos[:, ::-1])
        sin = np.ascontiguousarray(sin[:, ::-1])
    bf = ml_dtypes.bfloat16
    return {"k_ident": ident, "k_perm": perm.astype(bf), "k_bones": bones.astype(bf), "k_onesm": onesm,
            "k_cos": cos, "k_sin": sin, "k_mask": mask.astype(bf), "k_identb": ident.astype(bf)}


WSPEC = [("w_ada", [L, D, 6 * D]), ("b_ada", [L, 6 * D]), ("w_in", [L, D, INC]), ("a_sink", [L, 8]),
         ("b_q_gain", [L, 64]), ("b_k_gain", [L, 64]), ("c_conv_w", [L, 5, D]), ("c_conv_b", [L, D]),
         ("c_wr", [L, 2, 16, 64, 64]), ("c_br", [L, 2, D]), ("c_wi", [L, 2, 16, 64, 64]), ("c_bi", [L, 2, D]),
         ("c_lam", [L, 2, D]), ("w_br_a", [L, 512, D]), ("w_br_b", [L, 512, D]), ("w_br_c", [L, D, D]),
         ("w_out", [L, D, D]), ("ln1_g", [L, D]), ("ln1_b", [L, D]), ("w_ff1", [L, D, DFF]),
         ("w_ff2", [L, DFF, D]), ("ln2_g", [L, D]), ("ln2_b", [L, D])]


def build(dbg=(), nlayers=L, stop=None):
    nc = bass.Bass("TRN2", target_bir_lowering=False)
    es = ExitStack()
    dbg = set(dbg)

    def dram_in(name, shape, dt=F32):
        return nc.dram_tensor(name, list(shape), dt, kind="ExternalInput").ap()

    def scratch(name, shape, dt):
        kind = "ExternalOutput" if name in dbg else "Internal"
        return nc.dram_tensor(name, list(shape), dt, kind=kind).ap()

    x_in = dram_in("x", [NLAT, D])
    ctx_in = dram_in("ctx", [NCTX, D])
    cvec = dram_in("cvec", [2, D])
    W = {n: dram_in(n, s) for n, s in WSPEC}
    k_ident = dram_in("k_ident", [128, 128])
    k_perm = dram_in("k_perm", [128, 128], BF16)
    k_bones = dram_in("k_bones", [128, 128], BF16)
    k_onesm = dram_in("k_onesm", [128, 128])
    k_cos = dram_in("k_cos", [128, NLAT])
    k_sin = dram_in("k_sin", [128, NLAT])
    k_mask = dram_in("k_mask", [128, 2, 128], BF16)
    k_identb = dram_in("k_identb", [128, 128], BF16)
    out = nc.dram_tensor("out", [NLAT // 2, D], F32, kind="ExternalOutput").ap()

    wb_in = [scratch(f"wb_in{l}", [D, INC], BF16) for l in range(L)]
    wb_a = [scratch(f"wb_a{l}", [512, D], BF16) for l in range(L)]
    wb_b = [scratch(f"wb_b{l}", [512, D], BF16) for l in range(L)]
    wb_c = [scratch(f"wb_c{l}", [D, D], BF16) for l in range(L)]
    wb_o = [scratch(f"wb_o{l}", [D, D], BF16) for l in range(L)]
    wb_f1 = [scratch(f"wb_f1{l}", [D, DFF], BF16) for l in range(L)]
    wb_f2 = [scratch(f"wb_f2{l}", [DFF, D], BF16) for l in range(L)]
    xT = [scratch(f"xT{i}", [D, T], F32) for i in range(2)]
    hT = scratch("hT", [D, T], BF16)
    qAT = scratch("qAT", [512, T], BF16)
    qBT = scratch("qBT", [512, T], BF16)
    kAT = scratch("kAT", [128, T], BF16)
    kBT = scratch("kBT", [128, T], BF16)
    vAB = scratch("vAB", [T, 256], BF16)
    xrT = scratch("xrT", [D, T], F32)
    gyT = scratch("gyT", [D, T], F32)
    ycT = scratch("ycT", [D, T], BF16)
    dmod = scratch("dmod", [128, 96], F32)
    dx1 = scratch("dx1", [D, T], F32)
    dya = scratch("dya", [512, T], BF16)
    dyb = scratch("dyb", [512, T], BF16)
    dmt = scratch("dmt", [D, T], BF16)
    dh2 = scratch("dh2", [D, T], BF16)

    S = Sched(nc, es)

    uid = [0]

    def sb(stack, name, shape, dt, nsub=1):
        uid[0] += 1
        return Tl(stack.enter_context(nc.sbuf_tensor(f"{name}_{uid[0]}", list(shape), dt)), nsub)

    def ps(stack, name, shape, dt=F32, nsub=1):
        uid[0] += 1
        return Tl(stack.enter_context(nc.psum_tensor(f"{name}_{uid[0]}", list(shape), dt)), nsub, excl=True)

    def pool_of(stack, name, shape, dt, n, nsub=1, psum=False):
        mk = ps if psum else sb
        return Pool([mk(stack, f"{name}{i}", shape, dt, nsub) for i in range(n)])

    def kview(ap):
        return ap.rearrange("(k p) t -> p k t", p=128)

    ident = sb(es, "ident", [128, 128], F32)
    perm = sb(es, "perm", [128, 128], BF16)
    bones = sb(es, "bones", [128, 128], BF16)
    onesm = sb(es, "onesm", [128, 128], F32)
    PL = []
    for l_ in range(L):
        PL.append(dict(
            mod=sb(es, "mod", [128, 48, 2], F32),
            vecA=sb(es, "vecA", [128, 80], F32),
            vecB=sb(es, "vecB", [128, 96], F32),
            c12=sb(es, "c12", [128, 2, 16], F32),
            esink=sb(es, "esink", [128, 8], F32),
            gains=sb(es, "gains", [128, 2], F32)))
    S.dma("sp", ident[:, :], k_ident, writes=ident.b)
    S.dma("sp", perm[:, :], k_perm, writes=perm.b)
    S.dma("sp", bones[:, :], k_bones, writes=bones.b)
    S.dma("sp", onesm[:, :], k_onesm, writes=onesm.b)
    identb = sb(es, "identb", [128, 128], BF16)
    S.dma("sp", identb[:, :], k_identb, writes=identb.b)

    WB = {}
    castq = []

    def cast_dram(key, dst, src, rows, step):
        WB[key] = []
        for r0 in range(0, rows, step):
            b_ = Buf()
            WB[key].append(b_)
            castq.append((b_, dst[r0:r0 + step, :], src[r0:r0 + step, :]))

    def pump(k):
        for _ in range(min(k, len(castq))):
            b_, d_, s_ = castq.pop(0)
            S.dma("pool", d_, s_, writes=[b_], bg=False)

    for l in range(nlayers):
        cast_dram(("in", l), wb_in[l], W["w_in"][l], D, 256)
        cast_dram(("a", l), wb_a[l], W["w_br_a"][l], 512, 256)
        cast_dram(("b", l), wb_b[l], W["w_br_b"][l], 512, 256)
        cast_dram(("c", l), wb_c[l], W["w_br_c"][l], D, 256)
        cast_dram(("o", l), wb_o[l], W["w_out"][l], D, 256)
        cast_dram(("f1", l), wb_f1[l], W["w_ff1"][l], D, 256)
        cast_dram(("f2", l), wb_f2[l], W["w_ff2"][l], DFF, 512)
    pump(1000)

    def phase_M(l, ph):
        mod, vecA, vecB, c12, esink, gains = (PL[l][k_] for k_ in ("mod", "vecA", "vecB", "c12", "esink", "gains"))
        if True:
            s_raw = sb(ph, "s_raw", [128, 2, 8], F32)
            s_act = sb(ph, "s_act", [128, 2, 8], F32)
            rowsA = sb(ph, "rowsA", [80, 128], F32)
            rowsB = sb(ph, "rowsB", [96, 128], F32)
            wada = pool_of(ph, "wada", [128, 8, 1024], F32, 2)
            pm = ps(ph, "pm", [128, 512], F32)
            pv = ps(ph, "pv", [128, 512], F32)
            tmpc = sb(ph, "tmpc", [128, 16], F32)
            sk = sb(ph, "sk", [128, 8], F32)
            for j in range(2):
                S.dma("sp", s_raw[:, j, :], cvec[j].rearrange("(p k) -> p k", k=8), writes=s_raw.b)
            S.op("act", lambda e: e.activation(out=s_act[:, :, :], in_=s_raw[:, :, :], func=AF.Silu),
                 reads=s_raw.b, writes=s_act.b)
            S.dma("sp", rowsA[0:48, :], W["b_ada"][l].rearrange("(c p) -> c p", p=128), writes=rowsA.b)
            for i, nm in enumerate(("ln1_g", "ln1_b", "ln2_g", "ln2_b")):
                S.dma("sp", rowsA[48 + 8 * i:56 + 8 * i, :], W[nm][l].rearrange("(c p) -> c p", p=128), writes=rowsA.b)
            S.dma("sp", rowsB[0:40, :], W["c_conv_w"][l].rearrange("j (c p) -> (j c) p", p=128), writes=rowsB.b)
            S.dma("sp", rowsB[40:48, :], W["c_conv_b"][l].rearrange("(c p) -> c p", p=128), writes=rowsB.b)
            for i, nm in enumerate(("c_br", "c_bi", "c_lam")):
                S.dma("sp", rowsB[48 + 16 * i:64 + 16 * i, :], W[nm][l].rearrange("d (c p) -> (d c) p", p=128),
                      writes=rowsB.b)
            S.mm([lambda e: e.transpose(out=pv[:, 0:80], in_=rowsA[:, :], identity=ident[0:80, 0:80])],
                 reads=rowsA.b + ident.b, writes=pv.b)
            S.mm([lambda e: e.transpose(out=pv[:, 128:224], in_=rowsB[:, :], identity=ident[0:96, 0:96])],
                 reads=rowsB.b + ident.b, writes=pv.b)
            S.op("act", lambda e: e.activation(out=vecA[:, :], in_=pv[:, 0:80], func=AF.Copy), reads=pv.b, writes=vecA.b)
            S.op("act", lambda e: e.activation(out=vecB[:, :], in_=pv[:, 128:224], func=AF.Copy), reads=pv.b, writes=vecB.b)
            wav = W["w_ada"][l].rearrange("(p k) n -> p k n", k=8)
            for cc in range(6):
                wa = wada.get()
                S.dma("sp", wa[:, :, :], wav[:, :, cc * 1024:(cc + 1) * 1024], writes=wa.b)
                fns = []
                for oc in range(8):
                    for k in range(8):
                        fns.append(lambda e, oc=oc, k=k, wa=wa: e.matmul(
                            pm[:, (cc * 8 + oc) * 2:(cc * 8 + oc) * 2 + 2], lhsT=wa[:, k, oc * 128:(oc + 1) * 128],
                            rhs=s_act[:, :, k], start=(k == 0), stop=(k == 7)))
                S.mm(fns, reads=wa.b + s_act.b, writes=pm.b)
            for j in range(2):
                S.op("dve", lambda e, j=j: e.tensor_tensor(out=mod[:, :, j], in0=pm[:, 0:96].rearrange("p (c j) -> p c j", j=2)[:, :, j],
                                                           in1=vecA[:, 0:48], op=ALU.add), reads=pm.b + vecA.b, writes=mod.b)
            for g in (1, 4):
                S.op("dve", lambda e, g=g: e.tensor_scalar_add(out=mod[:, g * 8:(g + 1) * 8, :], in0=mod[:, g * 8:(g + 1) * 8, :],
                                                               scalar1=1.0), reads=mod.b, writes=mod.b)
            if "dmod" in dbg:
                S.dma("sp", dmod, mod[:, :, :].rearrange("p c j -> p (c j)"), reads=mod.b)
            S.op("act", lambda e: e.activation(out=tmpc[:, :], in_=vecB[:, 80:96], func=AF.Exp, scale=-1.0),
                 reads=vecB.b, writes=tmpc.b)
            S.op("act", lambda e: e.activation(out=tmpc[:, :], in_=tmpc[:, :], func=AF.Ln, bias=1.0, scale=1.0),
                 reads=tmpc.b, writes=tmpc.b)
            S.op("dve", lambda e: e.tensor_scalar_mul(out=c12[:, 0, :], in0=tmpc[:, :], scalar1=-8.0), reads=tmpc.b, writes=c12.b)
            S.op("dve", lambda e: e.tensor_scalar_mul(out=c12[:, 1, :], in0=tmpc[:, :], scalar1=-16.0), reads=tmpc.b, writes=c12.b)
            S.dma("sp", sk[:, :], W["a_sink"][l].partition_broadcast(128), writes=sk.b)
            S.op("act", lambda e: e.activation(out=esink[:, :], in_=sk[:, :], func=AF.Exp), reads=sk.b, writes=esink.b)
            for i, nm in enumerate(("b_q_gain", "b_k_gain")):
                for h in range(2):
                    S.dma("sp", gains[h * 64:(h + 1) * 64, i:i + 1], W[nm][l].rearrange("(p o) -> p o", o=1), writes=gains.b)

    ph0 = ExitStack()
    for l_ in range(nlayers):
        phase_M(l_, ph0)

    with ExitStack() as ph:
        xin = pool_of(ph, "xin", [128, D], F32, 2)
        xst = pool_of(ph, "xst", [128, 8, 128], F32, 2)
        pst = pool_of(ph, "pst", [128, 2, 512], F32, 2, nsub=2, psum=True)
        xTv = kview(xT[0])
        for blk in range(NBLK):
            src = ctx_in[blk * 128:(blk + 1) * 128, :] if blk < 2 else x_in[(blk - 2) * 128:(blk - 1) * 128, :]
            xi = xin.get()
            S.dma("sp", xi[:, :], src, writes=xi.b)
            pp = pst.get()
            S.mm([(lambda e, k=k: e.transpose(out=pp[:, k // 4, (k % 4) * 128:(k % 4 + 1) * 128],
                                               in_=xi[:, k * 128:(k + 1) * 128], identity=ident[:, :]))
                  for k in range(8)], reads=xi.b + ident.b, writes=pp.b)
            xs = xst.get()
            S.op("act", lambda e: e.activation(out=xs[:, 0:4, :], in_=pp[:, 0, :].rearrange("p (k t) -> p k t", k=4),
                                               func=AF.Copy), reads=[pp.b[0]], writes=xs.b)
            S.op("dve", lambda e: e.tensor_copy(out=xs[:, 4:8, :], in_=pp[:, 1, :].rearrange("p (k t) -> p k t", k=4)),
                 reads=[pp.b[1]], writes=xs.b)
            S.dma("sp", xTv[:, :, blk * 128:(blk + 1) * 128], xs[:, :, :], reads=xs.b)
    S.barrier()
    ph0.close()

    for l in range(nlayers):
        with_ctx = l < L - 1
        last = l == L - 1
        xT_cur, xT_nxt = xT[l % 2], xT[(l + 1) % 2]
        mod, vecA, vecB, c12, esink, gains = (PL[l][k_] for k_ in ("mod", "vecA", "vecB", "c12", "esink", "gains"))


        with ExitStack() as ph:
            W1 = sb(ph, "W1", [128, 8, 3584], BF16)
            cosT = sb(ph, "cosT", [128, NLAT], F32)
            sinT = sb(ph, "sinT", [128, NLAT], F32)
            xt_p = pool_of(ph, "xt", [128, 8, 512], F32, 2)
            ht_p = pool_of(ph, "ht", [128, 8, 512], BF16, 2)
            obf = pool_of(ph, "obf", [128, 512], BF16, 6)
            of32 = pool_of(ph, "of32", [128, 512], F32, 4)
            t32 = pool_of(ph, "t32", [128, 512], F32, 14)
            tb16 = pool_of(ph, "tb16", [128, 512], BF16, 8)
            vst = pool_of(ph, "vst", [128, 4, 256], BF16, 2)
            pmain = pool_of(ph, "pmain", [128, 512], F32, 5, psum=True)
            paux = pool_of(ph, "paux", [128, 512], F32, 3, psum=True)
            wv = kview(wb_in[l])
            for base, dst0 in ((0, 0), (768, 640)):
                for j in range(4):
                    S.dma("sp", W1[:, :, dst0 + j * 128:dst0 + j * 128 + 64], wv[:, :, base + j * 64:base + j * 64 + 64], reads=WB[("in", l)], writes=W1.b)
                    S.dma("sp", W1[:, :, dst0 + j * 128 + 64:dst0 + j * 128 + 128],
                          wv[:, :, base + (4 + j) * 64:base + (4 + j) * 64 + 64], reads=WB[("in", l)], writes=W1.b)
            S.dma("sp", W1[:, :, 512:640], wv[:, :, 512:640], reads=WB[("in", l)], writes=W1.b)
            S.dma("sp", W1[:, :, 1152:1280], wv[:, :, 1280:1408], reads=WB[("in", l)], writes=W1.b)
            S.dma("sp", W1[:, :, 1280:1408], wv[:, :, 640:768], reads=WB[("in", l)], writes=W1.b)
            S.dma("sp", W1[:, :, 1408:1536], wv[:, :, 1408:1536], reads=WB[("in", l)], writes=W1.b)
            S.dma("sp", W1[:, :, 1536:2560], wv[:, :, 1536:2560], reads=WB[("in", l)], writes=W1.b)
            S.dma("sp", W1[:, :, 2560:3584], wv[:, :, 2560:3584], reads=WB[("in", l)], writes=W1.b)
            S.dma("sp", cosT[:, :], k_cos, writes=cosT.b)
            S.dma("sp", sinT[:, :], k_sin, writes=sinT.b)
            xcv, hTv = kview(xT_cur), kview(hT)

            tiles = [(0, NCTX)] + [(NCTX + 512 * i, 512) for i in range(8)]
            if stop == "A0a":
                tiles = []
            if stop == "A0b":
                tiles = tiles[:1]
            if stop == "A0c":
                tiles = tiles[1:2]
            for (t0, n) in tiles:
                pump(4)
                is_ctx = t0 == 0
                mj = 1 if is_ctx else 0
                p0 = t0 - NCTX
                mine = (not last) or is_ctx or (t0 < NCTX + NLAT // 2)
                xt = xt_p.get()
                S.dma("sp", xt[:, :, :n], xcv[:, :, t0:t0 + n], writes=xt.b)
                ht = ht_p.get()
                for k in range(8):
                    S.op("dve", lambda e, k=k: e.tensor_scalar(out=ht[:, k, :n], in0=xt[:, k, :n], scalar1=mod[:, 8 + k, mj:mj + 1],
                                                               scalar2=mod[:, k, mj:mj + 1], op0=ALU.mult, op1=ALU.add),
                         reads=xt.b + mod.b, writes=ht.b)
                if mine:
                    S.dma("pool", hTv[:, :, t0:t0 + n], ht[:, :, :n], reads=ht.b)

                def proj(col):
                    pp = pmain.get()
                    S.mm([(lambda e, k=k: e.matmul(pp[:, :n], lhsT=W1[:, k, col:col + 128], rhs=ht[:, k, :n],
                                                   start=(k == 0), stop=(k == 7))) for k in range(8)],
                         reads=W1.b + ht.b, writes=pp.b)
                    return pp

                jobs = []
                if ((not is_ctx) or with_ctx) and mine:
                    jobs += [dict(col=j * 128, dst=qAT, r0=j * 128, g=None) for j in range(4)]
                jobs.append(dict(col=512, dst=kAT, r0=0, g=None))
                if ((not is_ctx) or with_ctx) and mine:
                    jobs += [dict(col=640 + j * 128, dst=qBT, r0=j * 128, g=0) for j in range(4)]
                jobs.append(dict(col=1152, dst=kBT, r0=0, g=1))
                do_rope = not (is_ctx or os.environ.get("NOROPE"))

                def st1(jb):
                    jb["pp"] = proj(jb["col"])

                def st2(jb):
                    if jb["g"] is None:
                        return
                    pp, sq, pa = jb["pp"], tb16.get(), paux.get()
                    S.op("act", lambda e: e.activation(out=sq[:, :n], in_=pp[:, :n], func=AF.Square), reads=pp.b, writes=sq.b)
                    S.mm([lambda e: e.matmul(pa[:, :n], lhsT=bones[:, :], rhs=sq[:, :n], start=True, stop=True)],
                         reads=bones.b + sq.b, writes=pa.b)
                    jb["ms"] = pa

                def st3(jb):
                    src = jb["pp"]
                    if jb["g"] is not None:
                        pa, rs, qn, gcol = jb["ms"], t32.get(), t32.get(), jb["g"]
                        S.op("dve", lambda e: e.tensor_scalar_add(out=rs[:, :n], in0=pa[:, :n], scalar1=RMS_EPS), reads=pa.b, writes=rs.b)
                        S.op("act", lambda e: e.activation(out=rs[:, :n], in_=rs[:, :n], func=AF.Ln), reads=rs.b, writes=rs.b)
                        S.op("act", lambda e: e.activation(out=rs[:, :n], in_=rs[:, :n], func=AF.Exp, scale=-0.5), reads=rs.b, writes=rs.b)
                        S.op("dve", lambda e: e.scalar_tensor_tensor(out=qn[:, :n], in0=src[:, :n], scalar=gains[:, gcol:gcol + 1],
                                                                     in1=rs[:, :n], op0=ALU.mult, op1=ALU.mult),
                             reads=src.b + gains.b + rs.b, writes=qn.b)
                        src = qn
                    if not do_rope:
                        ob = obf.get()
                        S.op("act", lambda e: e.activation(out=ob[:, :n], in_=src[:, :n], func=AF.Copy), reads=src.b, writes=ob.b)
                        S.dma("pool", jb["dst"][jb["r0"]:jb["r0"] + 128, t0:t0 + n], ob[:, :n], reads=ob.b)
                        return
                    qb, a1 = tb16.get(), t32.get()
                    S.op("act", lambda e: e.activation(out=qb[:, :n], in_=src[:, :n], func=AF.Copy), reads=src.b, writes=qb.b)
                    S.op("dve", lambda e: e.tensor_tensor(out=a1[:, :n], in0=src[:, :n], in1=cosT[:, p0:p0 + n], op=ALU.mult),
                         reads=src.b + cosT.b, writes=a1.b)
                    jb["qb"], jb["a1"] = qb, a1

                def st4(jb):
                    if not do_rope:
                        return
                    qb, pa = jb["qb"], paux.get()
                    S.mm([lambda e: e.matmul(pa[:, :n], lhsT=perm[:, :], rhs=qb[:, :n], start=True, stop=True)],
                         reads=perm.b + qb.b, writes=pa.b)
                    jb["pm"] = pa

                def st5(jb):
                    if not do_rope:
                        return
                    pa, a1, a2, ob = jb["pm"], jb["a1"], t32.get(), obf.get()
                    S.op("dve", lambda e: e.tensor_tensor(out=a2[:, :n], in0=pa[:, :n], in1=sinT[:, p0:p0 + n], op=ALU.mult),
                         reads=pa.b + sinT.b, writes=a2.b)
                    S.op("pool", lambda e: e.tensor_tensor(out=ob[:, :n], in0=a1[:, :n], in1=a2[:, :n], op=ALU.add),
                         reads=a1.b + a2.b, writes=ob.b)
                    S.dma("pool", jb["dst"][jb["r0"]:jb["r0"] + 128, t0:t0 + n], ob[:, :n], reads=ob.b)

                stages = (st1, st2, st3, st4, st5)
                for step in range(len(jobs) + len(stages) - 1):
                    for si, fn_ in enumerate(stages):
                        ji = step - si
                        if 0 <= ji < len(jobs):
                            fn_(jobs[ji])
                vs = vst.get()
                for blk in range(n // 128):
                    pp = pmain.get()
                    S.mm([(lambda e, k=k: e.matmul(pp[:, 0:256], lhsT=ht[:, k, blk * 128:(blk + 1) * 128], rhs=W1[:, k, 1280:1536],
                                                   start=(k == 0), stop=(k == 7))) for k in range(8)],
                         reads=W1.b + ht.b, writes=pp.b)
                    S.op("act", lambda e: e.activation(out=vs[:, blk, :], in_=pp[:, 0:256], func=AF.Copy), reads=pp.b, writes=vs.b)
                S.dma("pool", vAB[t0:t0 + n, :].rearrange("(b p) c -> p b c", p=128), vs[:, 0:n // 128, :], reads=vs.b)
                for c in range(8):
                    pp = proj(1536 + c * 128)
                    o = of32.get()
                    S.op("act", lambda e: e.activation(out=o[:, :n], in_=pp[:, :n], func=AF.Copy), reads=pp.b, writes=o.b)
                    S.dma("pool", xrT[c * 128:(c + 1) * 128, t0:t0 + n], o[:, :n], reads=o.b)
                if ((not is_ctx) or with_ctx) and mine:
                    for c in range(8):
                        pp = proj(2560 + c * 128)
                        a1, a2 = t32.get(), t32.get()
                        S.op("act", lambda e: e.activation(out=a1[:, :n], in_=pp[:, :n], func=AF.Square), reads=pp.b, writes=a1.b)
                        S.op("pool", lambda e: e.tensor_scalar(out=a1[:, :n], in0=a1[:, :n], scalar1=0.044715, scalar2=1.0,
                                                               op0=ALU.mult, op1=ALU.add), reads=a1.b, writes=a1.b)
                        S.op("dve", lambda e: e.tensor_tensor(out=a1[:, :n], in0=pp[:, :n], in1=a1[:, :n], op=ALU.mult),
                             reads=pp.b + a1.b, writes=a1.b)
                        S.op("act", lambda e: e.activation(out=a2[:, :n], in_=a1[:, :n], func=AF.Sigmoid, scale=1.5957691216057308),
                             reads=a1.b, writes=a2.b)
                        o = of32.get()
                        S.op("dve", lambda e: e.tensor_tensor(out=o[:, :n], in0=pp[:, :n], in1=a2[:, :n], op=ALU.mult),
                             reads=pp.b + a2.b, writes=o.b)
                        S.dma("pool", gyT[c * 128:(c + 1) * 128, t0:t0 + n], o[:, :n], reads=o.b)
        S.barrier()
        if stop in (f"A{l}", "A0a", "A0b", "A0c"):
            break

        with ExitStack() as ph:
            Wg = sb(ph, "Wg", [128, 2, 2, 8, 128], BF16)
            xr_p = pool_of(ph, "xr", [128, T], F32, 1)
            gy_p = pool_of(ph, "gy", [128, T], F32, 1)
            xc = sb(ph, "xc", [128, T], F32)
            xcb = sb(ph, "xcb", [128, T], BF16)
            a_t = sb(ph, "a_t", [128, T], F32)
            u_t = sb(ph, "u_t", [128, T], F32)
            hf = sb(ph, "hf", [128, T], F32)
            hb = sb(ph, "hb", [128, T], F32)
            ycb = sb(ph, "ycb", [128, T], BF16)
            r_f = sb(ph, "r_f", [128, T], F32)
            i_f = sb(ph, "i_f", [128, T], F32)
            pg = pool_of(ph, "pg", [128, 512], F32, 6, psum=True)
            S.op("pool", lambda e: e.memset(Wg[:, :, :, :, :], 0.0), writes=Wg.b)
            for d in range(2):
                for gi, nm in enumerate(("c_wr", "c_wi")):
                    for par in range(2):
                        src = W[nm][l, d].rearrange("(c two) di e -> two di c e", two=2)[par]
                        S.dma("pool", Wg[par * 64:(par + 1) * 64, d, gi, :, par * 64:(par + 1) * 64], src, writes=Wg.b)
            ctiles = [(0, NCTX)] + [(NCTX + 512 * i, 512) for i in range(8)]
            for c in range(8):
                pump(3)
                xr = xr_p.get()
                gy = gy_p.get()
                S.dma("sp", xr[:, :], xrT[c * 128:(c + 1) * 128, :], writes=xr.b)
                ylo, yhi = (NCTX, NCTX + NLAT // 2) if last else (0, T)
                S.dma("sp", gy[:, ylo:yhi], gyT[c * 128:(c + 1) * 128, ylo:yhi], writes=gy.b)
                w = [vecB[:, j * 8 + c:j * 8 + c + 1] for j in range(5)]
                cb = vecB[:, 40 + c:41 + c]
                for (s, e_) in ((0, NCTX), (NCTX, T)):
                    S.op("act", lambda e, s=s, e_=e_: e.activation(out=xc[:, s:e_], in_=xr[:, s:e_], func=AF.Identity,
                                                                   scale=w[2], bias=cb), reads=xr.b + vecB.b, writes=xc.b)
                    for (j, so, do, ln) in ((0, s, s + 2, e_ - s - 2), (1, s, s + 1, e_ - s - 1), (3, s + 1, s, e_ - s - 1),
                                            (4, s + 2, s, e_ - s - 2)):
                        S.op("dve", lambda e, j=j, so=so, do=do, ln=ln: e.scalar_tensor_tensor(
                            out=xc[:, do:do + ln], in0=xr[:, so:so + ln], scalar=w[j], in1=xc[:, do:do + ln],
                            op0=ALU.mult, op1=ALU.add), reads=xr.b + xc.b + vecB.b, writes=xc.b)
                S.op("act", lambda e: e.activation(out=xcb[:, :], in_=xc[:, :], func=AF.Copy), reads=xc.b, writes=xcb.b)
                for d in range(2):
                    br = vecB[:, 48 + d * 8 + c:49 + d * 8 + c]
                    bi = vecB[:, 64 + d * 8 + c:65 + d * 8 + c]
                    c1 = c12[:, 0, d * 8 + c:d * 8 + c + 1]
                    c2 = c12[:, 1, d * 8 + c:d * 8 + c + 1]
                    TE = (NCTX + NLAT // 2) if (last and d == 0) else T
                    for (t0, n) in ctiles:
                        if t0 >= TE:
                            continue
                        pr, pi = pg.get(), pg.get()
                        S.mm([lambda e: e.matmul(pr[:, :n], lhsT=Wg[:, d, 0, c, :], rhs=xcb[:, t0:t0 + n], start=True, stop=True)],
                             reads=Wg.b + xcb.b, writes=pr.b)
                        S.mm([lambda e: e.matmul(pi[:, :n], lhsT=Wg[:, d, 1, c, :], rhs=xcb[:, t0:t0 + n], start=True, stop=True)],
                             reads=Wg.b + xcb.b, writes=pi.b)
                        S.op("act", lambda e: e.activation(out=r_f[:, t0:t0 + n], in_=pr[:, :n], func=AF.Sigmoid, bias=br),
                             reads=pr.b + vecB.b, writes=r_f.b)
                        S.op("act", lambda e: e.activation(out=i_f[:, t0:t0 + n], in_=pi[:, :n], func=AF.Sigmoid, bias=bi),
                             reads=pi.b + vecB.b, writes=i_f.b)
                    S.op("dve", lambda e: e.tensor_tensor(out=i_f[:, :TE], in0=i_f[:, :TE], in1=xc[:, :TE], op=ALU.mult),
                         reads=i_f.b + xc.b, writes=i_f.b)
                    S.op("act", lambda e: e.activation(out=a_t[:, :TE], in_=r_f[:, :TE], func=AF.Exp, scale=c1),
                         reads=r_f.b + c12.b, writes=a_t.b)
                    S.op("act", lambda e: e.activation(out=r_f[:, :TE], in_=r_f[:, :TE], func=AF.Exp, scale=c2),
                         reads=r_f.b + c12.b, writes=r_f.b)
                    S.op("act", lambda e: e.activation(out=r_f[:, :TE], in_=r_f[:, :TE], func=AF.Sqrt, scale=-1.0, bias=1.0),
                         reads=r_f.b, writes=r_f.b)
                    S.op("dve", lambda e: e.tensor_tensor(out=u_t[:, :TE], in0=r_f[:, :TE], in1=i_f[:, :TE], op=ALU.mult),
                         reads=r_f.b + i_f.b, writes=u_t.b)
                    if d == 0:
                        S.op("dve", lambda e: e.tensor_tensor_scan(out=hf[:, :TE], data0=a_t[:, :TE], data1=u_t[:, :TE], initial=0.0,
                                                                   op0=ALU.mult, op1=ALU.add), reads=a_t.b + u_t.b, writes=hf.b)
                    else:
                        S.op("dve", lambda e: e.tensor_tensor_scan(out=hb[:, 0:NCTX][:, ::-1], data0=a_t[:, 0:NCTX][:, ::-1],
                                                                   data1=u_t[:, 0:NCTX][:, ::-1], initial=0.0,
                                                                   op0=ALU.mult, op1=ALU.add), reads=a_t.b + u_t.b, writes=hb.b)
                        S.op("dve", lambda e: e.tensor_tensor_scan(out=hb[:, NCTX:T][:, ::-1], data0=a_t[:, NCTX:T][:, ::-1],
                                                                   data1=u_t[:, NCTX:T][:, ::-1], initial=hb[:, 0:1],
                                                                   op0=ALU.mult, op1=ALU.add), reads=a_t.b + u_t.b + hb.b, writes=hb.b)
                S.op("pool", lambda e: e.tensor_tensor(out=hf[:, ylo:yhi], in0=hf[:, ylo:yhi], in1=hb[:, ylo:yhi], op=ALU.add),
                     reads=hf.b + hb.b, writes=hf.b)
                S.op("dve", lambda e: e.tensor_tensor(out=ycb[:, ylo:yhi], in0=hf[:, ylo:yhi], in1=gy[:, ylo:yhi], op=ALU.mult),
                     reads=hf.b + gy.b, writes=ycb.b)
                S.dma("pool", ycT[c * 128:(c + 1) * 128, ylo:yhi], ycb[:, ylo:yhi], reads=ycb.b)
        S.barrier()
        if stop == f"C{l}":
            break

        n = NF
        nft = NLAT // n // 2 if last else NLAT // n
        ftiles = ([(0, True)] if with_ctx else []) + [(NCTX + n * i, False) for i in range(nft)]
        xcv, xnv, hTv, ycv = kview(xT_cur), kview(xT_nxt), kview(hT), kview(ycT)
        qAv, qBv, yav, ybv = kview(qAT), kview(qBT), kview(dya), kview(dyb)
        x1v, h2v = kview(dx1), kview(dh2)
        with ExitStack() as ph:
            kA = sb(ph, "kA", [128, T], BF16)
            kB = sb(ph, "kB", [128, T], BF16)
            Vg = sb(ph, "Vg", [128, NBLK, 4, 128], BF16)
            mask = sb(ph, "mask", [128, 2, 128], BF16)
            qA_p = pool_of(ph, "qA", [128, 4, 2, n], BF16, 2)
            qB_p = pool_of(ph, "qB", [128, 4, 2, n], BF16, 2)
            pT_p = pool_of(ph, "pT", [128, 2, 2 * n], BF16, 4)
            pA_p = pool_of(ph, "pA", [128, 640], BF16, 4)
            rd_p = pool_of(ph, "rd", [128, n], F32, 3)
            ya_p = pool_of(ph, "yaT", [128, 4, n], BF16, 2, nsub=4)
            yb_p = pool_of(ph, "ybT", [128, 4, n], BF16, 2, nsub=4)
            pS = pool_of(ph, "pS", [128, 2, 512], F32, 2, psum=True)
            pO = pool_of(ph, "pO", [128, 512], F32, 4, psum=True)

            for tq in qA_p.tiles + qB_p.tiles:
                S.op("pool", lambda e: e.memset(tq[:, :, :, :], 0.0), writes=tq.b)
            S.dma("sp", kA[:, :], kAT, writes=kA.b)
            S.dma("sp", kB[:, :], kBT, writes=kB.b)
            S.dma("sp", mask[:, :, :], k_mask, writes=mask.b)
            S.op("pool", lambda e: e.memset(Vg[:, :, :, :], 1.0), writes=Vg.b)
            vsrc = vAB.rearrange("(b p) (m e) -> p b m e", p=128, e=64)
            for m_ in range(4):
                off = 0 if m_ % 2 == 0 else 64
                for b0 in range(0, NBLK, 9):
                    b1 = min(NBLK, b0 + 9)
                    S.dma("sp", Vg[:, b0:b1, m_, off:off + 64], vsrc[:, b0:b1, m_, :], writes=Vg.b)

            for (t0, is_ctx) in ftiles:
                qA, qB = qA_p.get(), qB_p.get()
                yaT, ybT = ya_p.get(), yb_p.get()
                for (qt, qv) in ((qB, qBv), (qA, qAv)):
                    S.dma("sp", qt[0:64, :, 0, :], qv[0:64, :, t0:t0 + n], writes=qt.b)
                    S.dma("sp", qt[64:128, :, 1, :], qv[64:128, :, t0:t0 + n], writes=qt.b)

                kblocks = [0, 1] if is_ctx else list(range(NBLK))
                groups = [kblocks[i:i + 2] for i in range(0, len(kblocks), 2)]
                for j in range(4):
                    po = [pO.get(), pO.get()]
                    qflat = qB[:, j, :, :].rearrange("p a b -> p (a b)")

                    def pv_b(gi, g, pt2):
                        fns = []
                        for i, kb in enumerate(g):
                            first = (gi == 0 and i == 0)
                            lastf = (gi == len(groups) - 1 and i == len(g) - 1)
                            fns.append(lambda e, i=i, kb=kb, first=first, lastf=lastf: e.matmul(
                                po[0][:, :n], lhsT=Vg[:, kb, 2, :], rhs=pt2[:, i, 0:n], start=first, stop=lastf))
                            fns.append(lambda e, i=i, kb=kb, first=first, lastf=lastf: e.matmul(
                                po[1][:, :n], lhsT=Vg[:, kb, 3, :], rhs=pt2[:, i, n:2 * n], start=first, stop=lastf))
                        S.mm(fns, reads=Vg.b + pt2.b, writes=po[0].b + po[1].b)

                    prev = None
                    for gi, g in enumerate(groups):
                        pst_ = pS.get()
                        S.mm([(lambda e, i=i, kb=kb: e.matmul(pst_[:, i, :], lhsT=kB[:, kb * 128:(kb + 1) * 128], rhs=qflat,
                                                             start=True, stop=True)) for i, kb in enumerate(g)],
                             reads=kB.b + qB.b, writes=pst_.b)
                        pt = pT_p.get()
                        S.op("act", lambda e: e.activation(out=pt[:, 0:len(g), :], in_=pst_[:, 0:len(g), :], func=AF.Exp, scale=0.125),
                             reads=pst_.b, writes=pt.b)
                        if prev is not None:
                            pv_b(*prev)
                        prev = (gi, g, pt)
                    pv_b(*prev)
                    for hh in range(2):
                        orow = slice(0, 64) if hh == 0 else slice(64, 128)
                        drow = slice(64, 128) if hh == 0 else slice(0, 64)
                        rd = rd_p.get()
                        S.op("dve", lambda e: e.reciprocal(out=rd[drow, :], in_=po[hh][drow, :n]), reads=po[hh].b, writes=rd.b)
                        S.op("dve", lambda e: e.tensor_tensor(out=ybT[orow, j, :], in0=po[hh][orow, :n], in1=rd[drow, :], op=ALU.mult),
                             reads=po[hh].b + rd.b, writes=[ybT.b[j]])
                S.dma("pool", ybv[:, :, t0:t0 + n], ybT[:, :, :], reads=ybT.b)

                nqb = n // 128
                prevA = None

                def pv_a(po, hh, qb_, kbs, pa, fin):
                    S.mm([(lambda e, s=s, kb=kb: e.matmul(po[:, qb_ * 128:(qb_ + 1) * 128], lhsT=Vg[:, kb, hh, :],
                                                          rhs=pa[:, s * 128:(s + 1) * 128], start=(s == 0), stop=(s == len(kbs) - 1)))
                          for s, (kb, _) in enumerate(kbs)], reads=Vg.b + pa.b, writes=po.b)
                    if fin is not None:
                        j, h = fin
                        orow = slice(0, 64) if hh == 0 else slice(64, 128)
                        drow = slice(64, 128) if hh == 0 else slice(0, 64)
                        rd = rd_p.get()
                        S.op("dve", lambda e: e.tensor_scalar(out=rd[drow, :], in0=po[drow, :n], scalar1=esink[drow, h:h + 1], scalar2=None,
                                                              op0=ALU.add), reads=po.b + esink.b, writes=rd.b)
                        S.op("dve", lambda e: e.reciprocal(out=rd[drow, :], in_=rd[drow, :]), reads=rd.b, writes=rd.b)
                        S.op("dve", lambda e: e.tensor_tensor(out=yaT[orow, j, :], in0=po[orow, :n], in1=rd[drow, :], op=ALU.mult),
                             reads=po.b + rd.b, writes=[yaT.b[j]])

                for j in range(4):
                    for hh in range(2):
                        h = j + 4 * hh
                        po = pO.get()
                        for qb_ in range(nqb):
                            if is_ctx:
                                kbs = [(0, None), (1, None)]
                            else:
                                B = (t0 - NCTX) // 128 + qb_
                                kbs = [(0, None), (1, None)]
                                if B - 1 >= 0:
                                    kbs.append((B - 1 + 2, 0))
                                kbs.append((B + 2, None))
                                if B + 1 < NLAT // 128:
                                    kbs.append((B + 1 + 2, 1))
                            pst_ = pS.get()
                            pflat = pst_[:, :, :].rearrange("p a b -> p (a b)")
                            fns = []
                            for s, (kb, mi) in enumerate(kbs):
                                fns.append(lambda e, s=s, kb=kb, mi=mi: e.matmul(
                                    pflat[:, s * 128:(s + 1) * 128], lhsT=kA[:, kb * 128:(kb + 1) * 128],
                                    rhs=qA[:, j, hh, qb_ * 128:(qb_ + 1) * 128], start=True, stop=(mi is None)))
                                if mi is not None:
                                    fns.append(lambda e, s=s, mi=mi: e.matmul(
                                        pflat[:, s * 128:(s + 1) * 128], lhsT=identb[:, :], rhs=mask[:, mi, :], start=False, stop=True))
                            S.mm(fns, reads=kA.b + qA.b + identb.b + mask.b, writes=pst_.b)
                            pa = pA_p.get()
                            nk = len(kbs) * 128
                            S.op("act", lambda e: e.activation(out=pa[:, 0:nk], in_=pflat[:, 0:nk], func=AF.Exp, scale=0.125),
                                 reads=pst_.b, writes=pa.b)
                            if prevA is not None:
                                pv_a(*prevA)
                            prevA = (po, hh, qb_, kbs, pa, (j, h) if qb_ == nqb - 1 else None)
                pv_a(*prevA)
                S.dma("pool", yav[:, :, t0:t0 + n], yaT[:, :, :], reads=yaT.b)
        S.barrier()
        if stop == f"F1{l}":
            break

        def layer_norm(xres, h2, pO, ln_p, lmean, lvar, mj, gofs, bofs, sc_g, sh_g):
            pmean, pex2 = pO.get(), pO.get()
            for oc in range(8):
                sq = ln_p.get()
                S.op("act", lambda e: e.activation(out=sq[:, :], in_=xres[:, oc, :], func=AF.Square),
                     reads=[xres.b[oc]], writes=sq.b)
                S.mm([lambda e: e.matmul(pmean[:, :n], lhsT=onesm[:, :], rhs=xres[:, oc, :], start=(oc == 0), stop=(oc == 7))],
                     reads=onesm.b + [xres.b[oc]], writes=pmean.b)
                S.mm([lambda e: e.matmul(pex2[:, :n], lhsT=onesm[:, :], rhs=sq[:, :], start=(oc == 0), stop=(oc == 7))],
                     reads=onesm.b + sq.b, writes=pex2.b)
            mean, var = lmean, lvar
            S.op("act", lambda e: e.activation(out=mean[:, :], in_=pmean[:, :n], func=AF.Copy), reads=pmean.b, writes=mean.b)
            S.op("act", lambda e: e.activation(out=var[:, :], in_=pmean[:, :n], func=AF.Square), reads=pmean.b, writes=var.b)
            S.op("dve", lambda e: e.tensor_tensor(out=var[:, :], in0=pex2[:, :n], in1=var[:, :], op=ALU.subtract),
                 reads=pex2.b + var.b, writes=var.b)
            S.op("dve", lambda e: e.tensor_scalar_add(out=var[:, :], in0=var[:, :], scalar1=LN_EPS), reads=var.b, writes=var.b)
            S.op("act", lambda e: e.activation(out=var[:, :], in_=var[:, :], func=AF.Sqrt), reads=var.b, writes=var.b)
            S.op("dve", lambda e: e.reciprocal(out=var[:, :], in_=var[:, :]), reads=var.b, writes=var.b)
            for oc in range(8):
                tt = ln_p.get()
                S.op("pool", lambda e: e.tensor_tensor(out=tt[:, :], in0=xres[:, oc, :], in1=mean[:, :], op=ALU.subtract),
                     reads=[xres.b[oc]] + mean.b, writes=tt.b)
                S.op("pool", lambda e: e.tensor_tensor(out=tt[:, :], in0=tt[:, :], in1=var[:, :], op=ALU.mult),
                     reads=tt.b + var.b, writes=tt.b)
                S.op("act", lambda e: e.activation(out=xres[:, oc, :], in_=tt[:, :], func=AF.Identity,
                                                   scale=vecA[:, gofs + oc:gofs + oc + 1], bias=vecA[:, bofs + oc:bofs + oc + 1]),
                     reads=tt.b + vecA.b, writes=[xres.b[oc]])
                if h2 is not None:
                    S.op("dve", lambda e: e.tensor_scalar(out=h2[:, oc, :], in0=xres[:, oc, :], scalar1=mod[:, sc_g * 8 + oc, mj:mj + 1],
                                                          scalar2=mod[:, sh_g * 8 + oc, mj:mj + 1], op0=ALU.mult, op1=ALU.add),
                         reads=[xres.b[oc]] + mod.b, writes=[h2.b[oc]])

        def resid(xres, oc, pp, gidx, mj):
            S.op("act", lambda e: e.activation(out=xres[:, oc, :], in_=xres[:, oc, :], func=AF.Copy, scale=float(ALPHA)),
                 reads=[xres.b[oc]], writes=[xres.b[oc]])
            S.op("dve", lambda e: e.scalar_tensor_tensor(out=xres[:, oc, :], in0=pp[:, :n], scalar=mod[:, gidx * 8 + oc, mj:mj + 1],
                                                         in1=xres[:, oc, :], op0=ALU.mult, op1=ALU.add),
                 reads=pp.b + [xres.b[oc]] + mod.b, writes=[xres.b[oc]])

        with ExitStack() as ph:
            Wgt = sb(ph, "Wgt", [128, 8, 3072], BF16, nsub=6)
            Wa = sb(ph, "Wa", [128, 4, D], BF16)
            Wb = sb(ph, "Wb", [128, 4, D], BF16)
            Wc = sb(ph, "Wc", [128, 8, D], BF16)
            Wo = sb(ph, "Wo", [128, 8, D], BF16)
            ya_p = pool_of(ph, "ya2", [128, 4, n], BF16, 2)
            yb_p = pool_of(ph, "yb2", [128, 4, n], BF16, 2)
            yc_p = pool_of(ph, "yc2", [128, 8, n], BF16, 2)
            ht_p = pool_of(ph, "ht2", [128, 8, n], BF16, 2)
            xr_p = pool_of(ph, "xres2", [128, 8, n], F32, 2, nsub=8)
            h2_p = pool_of(ph, "h22", [128, 8, n], BF16, 2, nsub=8)
            mT_p = pool_of(ph, "mT", [128, 8, n], BF16, 1, nsub=8)
            sg_p = pool_of(ph, "sg", [128, n], F32, 4)
            mt_p = pool_of(ph, "mt", [128, n], F32, 6)
            ln_p = pool_of(ph, "lnt", [128, n], F32, 6)
            lmean = sb(ph, "lmean", [128, n], F32)
            lvar = sb(ph, "lvar", [128, n], F32)
            pO = pool_of(ph, "pO2", [128, 512], F32, 8, psum=True)
            wiv = kview(wb_in[l])
            for gi in range(3):
                for hlf in range(2):
                    c0 = 3584 + gi * 1024 + hlf * 512
                    S.dma("sp", Wgt[:, :, gi * 1024 + hlf * 512:gi * 1024 + (hlf + 1) * 512], wiv[:, :, c0:c0 + 512], reads=WB[("in", l)], writes=[Wgt.b[gi * 2 + hlf]])
            for (wt, wsrc, wk) in ((Wa, wb_a[l], "a"), (Wb, wb_b[l], "b")):
                hv = wsrc.rearrange("(h d) n -> d h n", d=64)
                S.dma("sp", wt[0:64, :, :], hv[:, 0:4, :], reads=WB[(wk, l)], writes=wt.b)
                S.dma("sp", wt[64:128, :, :], hv[:, 4:8, :], reads=WB[(wk, l)], writes=wt.b)
            for hlf in range(2):
                S.dma("sp", Wc[:, :, hlf * 512:(hlf + 1) * 512], kview(wb_c[l])[:, :, hlf * 512:(hlf + 1) * 512], reads=WB[("c", l)], writes=Wc.b)
                S.dma("sp", Wo[:, :, hlf * 512:(hlf + 1) * 512], kview(wb_o[l])[:, :, hlf * 512:(hlf + 1) * 512], reads=WB[("o", l)], writes=Wo.b)
            for (t0, is_ctx) in ftiles:
                mj = 1 if is_ctx else 0
                yaT, ybT, ycs, hts = ya_p.get(), yb_p.get(), yc_p.get(), ht_p.get()
                xres, h2, mT = xr_p.get(), h2_p.get(), mT_p.get()
                S.dma("sp", hts[:, :, :], hTv[:, :, t0:t0 + n], writes=hts.b)
                S.dma("sp", yaT[:, :, :], yav[:, :, t0:t0 + n], writes=yaT.b)
                S.dma("sp", ybT[:, :, :], ybv[:, :, t0:t0 + n], writes=ybT.b)
                S.dma("sp", ycs[:, :, :], ycv[:, :, t0:t0 + n], writes=ycs.b)
                S.dma("sp", xres[:, :, :], xcv[:, :, t0:t0 + n], writes=xres.b)
                for oc in range(8):
                    cs = slice(oc * 128, (oc + 1) * 128)
                    acc = None
                    for gi, (wt, ysrc, nk) in enumerate(((Wa, yaT, 4), (Wb, ybT, 4), (Wc, ycs, 8))):
                        pgate = pO.get()
                        S.mm([(lambda e, k=k: e.matmul(pgate[:, :n], lhsT=Wgt[:, k, gi * 1024 + oc * 128:gi * 1024 + (oc + 1) * 128],
                                                       rhs=hts[:, k, :], start=(k == 0), stop=(k == 7))) for k in range(8)],
                             reads=[Wgt.b[gi * 2 + oc // 4]] + hts.b, writes=pgate.b)
                        sg = sg_p.get()
                        S.op("act", lambda e: e.activation(out=sg[:, :], in_=pgate[:, :n], func=AF.Sigmoid), reads=pgate.b, writes=sg.b)
                        pbr = pO.get()
                        S.mm([(lambda e, k=k: e.matmul(pbr[:, :n], lhsT=wt[:, k, cs], rhs=ysrc[:, k, :],
                                                       start=(k == 0), stop=(k == nk - 1))) for k in range(nk)],
                             reads=wt.b + ysrc.b, writes=pbr.b)
                        mt = mt_p.get()
                        S.op("dve", lambda e: e.tensor_tensor(out=mt[:, :], in0=pbr[:, :n], in1=sg[:, :], op=ALU.mult),
                             reads=pbr.b + sg.b, writes=mt.b)
                        if acc is None:
                            acc = mt
                        elif gi == 1:
                            S.op("pool", lambda e: e.tensor_tensor(out=mt[:, :], in0=mt[:, :], in1=acc[:, :], op=ALU.add),
                                 reads=mt.b + acc.b, writes=mt.b)
                            acc = mt
                        else:
                            S.op("dve", lambda e: e.tensor_tensor(out=mT[:, oc, :], in0=mt[:, :], in1=acc[:, :], op=ALU.add),
                                 reads=mt.b + acc.b, writes=[mT.b[oc]])
                for oc in range(8):
                    pp = pO.get()
                    S.mm([(lambda e, k=k: e.matmul(pp[:, :n], lhsT=Wo[:, k, oc * 128:(oc + 1) * 128], rhs=mT[:, k, :],
                                                   start=(k == 0), stop=(k == 7))) for k in range(8)],
                         reads=Wo.b + mT.b, writes=pp.b)
                    resid(xres, oc, pp, 2, mj)
                layer_norm(xres, h2, pO, ln_p, lmean, lvar, mj, 48, 56, 4, 3)
                S.dma("pool", x1v[:, :, t0:t0 + n], xres[:, :, :], reads=xres.b)
                S.dma("pool", h2v[:, :, t0:t0 + n], h2[:, :, :], reads=h2.b)
        S.barrier()
        if stop == f"F2{l}":
            break

        with ExitStack() as ph:
            W1f = sb(ph, "W1f", [128, 8, DFF], BF16, nsub=4)
            W2f = sb(ph, "W2f", [128, 32, D], BF16, nsub=4)
            xr_p = pool_of(ph, "xres3", [128, 8, n], F32, 2, nsub=8)
            h2_p = pool_of(ph, "h23", [128, 8, n], BF16, 2)
            fT = sb(ph, "fT", [128, 32, n], BF16, nsub=32)
            rl_p = pool_of(ph, "rl", [128, n], F32, 4)
            ln_p = pool_of(ph, "lnt3", [128, n], F32, 4)
            lmean = sb(ph, "lmean3", [128, n], F32)
            lvar = sb(ph, "lvar3", [128, n], F32)
            ost_p = pool_of(ph, "ost", [128, D], F32, 2)
            pO = pool_of(ph, "pO3", [128, 512], F32, 6, psum=True)
            pT2 = pool_of(ph, "pT2", [128, 2, 512], F32, 1, psum=True)
            w1v, w2v = kview(wb_f1[l]), kview(wb_f2[l])
            for q4 in range(4):
                S.dma("sp", W1f[:, :, q4 * 1024:(q4 + 1) * 1024], w1v[:, :, q4 * 1024:(q4 + 1) * 1024], reads=WB[("f1", l)], writes=[W1f.b[q4]])
            for q4 in range(4):
                S.dma("sp", W2f[:, q4 * 8:(q4 + 1) * 8, :], w2v[:, q4 * 8:(q4 + 1) * 8, :], reads=WB[("f2", l)], writes=[W2f.b[q4]])
            for (t0, is_ctx) in ftiles:
                mj = 1 if is_ctx else 0
                xres, h2 = xr_p.get(), h2_p.get()
                S.dma("sp", h2[:, :, :], h2v[:, :, t0:t0 + n], writes=h2.b)
                S.dma("sp", xres[:, :, :], x1v[:, :, t0:t0 + n], writes=xres.b)
                for fc in range(32):
                    pp = pO.get()
                    S.mm([(lambda e, k=k: e.matmul(pp[:, :n], lhsT=W1f[:, k, fc * 128:(fc + 1) * 128], rhs=h2[:, k, :],
                                                   start=(k == 0), stop=(k == 7))) for k in range(8)],
                         reads=[W1f.b[fc // 8]] + h2.b, writes=pp.b)
                    rl = rl_p.get()
                    S.op("act", lambda e: e.activation(out=rl[:, :], in_=pp[:, :n], func=AF.Relu), reads=pp.b, writes=rl.b)
                    S.op("dve" if fc % 2 == 0 else "pool", lambda e: e.tensor_tensor(out=fT[:, fc, :], in0=rl[:, :], in1=rl[:, :], op=ALU.mult),
                         reads=rl.b, writes=[fT.b[fc]])
                for oc in range(8):
                    pp = pO.get()
                    S.mm([(lambda e, k=k: e.matmul(pp[:, :n], lhsT=W2f[:, k, oc * 128:(oc + 1) * 128], rhs=fT[:, k, :],
                                                   start=(k == 0), stop=(k == 31))) for k in range(32)],
                         reads=W2f.b + fT.b, writes=pp.b)
                    resid(xres, oc, pp, 5, mj)
                layer_norm(xres, None, pO, ln_p, lmean, lvar, mj, 64, 72, 0, 0)
                if not last:
                    S.dma("pool", xnv[:, :, t0:t0 + n], xres[:, :, :], reads=xres.b)
                else:
                    for blk in range(n // 128):
                        ptp = pT2.get()
                        S.mm([(lambda e, k=k: e.transpose(out=ptp[:, k // 4, (k % 4) * 128:(k % 4 + 1) * 128],
                                                          in_=xres[:, k, blk * 128:(blk + 1) * 128], identity=ident[:, :])) for k in range(8)],
                             reads=xres.b + ident.b, writes=ptp.b)
                        ost = ost_p.get()
                        S.op("act", lambda e: e.activation(out=ost[:, 0:512], in_=ptp[:, 0, :], func=AF.Copy), reads=ptp.b, writes=ost.b)
                        S.op("dve", lambda e: e.tensor_copy(out=ost[:, 512:1024], in_=ptp[:, 1, :]), reads=ptp.b, writes=ost.b)
                        r0 = t0 - NCTX + blk * 128
                        S.dma("pool", out[r0:r0 + 128, :], ost[:, :], reads=ost.b)
        S.barrier()
        if stop == f"F{l}":
            break

    pump(1000)
    S.barrier(final=True)
    es.close()
    return nc


def make_in_maps(inputs, ncores=8):
    f = lambda a: np.ascontiguousarray(np.asarray(a, dtype=np.float32))
    shared = {n: f(inputs[n]) for n, _ in WSPEC if n not in ("c_conv_w", "c_wr", "c_br", "c_wi", "c_bi", "c_lam")}
    cw = f(inputs["c_conv_w"])
    z = np.zeros_like(cw[:, :1])
    par = []
    for rev in (False, True):
        d = {}
        d["c_conv_w"] = np.ascontiguousarray(np.concatenate([cw[:, ::-1], z], 1) if rev else np.concatenate([z, cw], 1))
        for n in ("c_wr", "c_br", "c_wi", "c_bi", "c_lam"):
            a = f(inputs[n])
            d[n] = np.ascontiguousarray(a[:, ::-1]) if rev else a
        d.update(_consts(rev))
        par.append(d)
    maps = []
    for core in range(ncores):
        b, rev = (core // 2) % 4, core % 2
        xb, cb = f(inputs["x"][b]), f(inputs["ctx"][b])
        if rev:
            xb, cb = np.ascontiguousarray(xb[::-1]), np.ascontiguousarray(cb[::-1])
        m = {"x": xb, "ctx": cb,
             "cvec": f(np.stack([np.asarray(inputs["c"][b]), np.asarray(inputs["c_ctx"])], 0))}
        m.update(shared)
        m.update(par[rev])
        maps.append(m)
    return maps


def kernel(**inputs):
    nc = build()
    maps = make_in_maps(inputs)
    res = run_bass_kernel_spmd(nc, maps, core_ids=list(range(8)))
    outs = []
    for b in range(4):
        lo = np.asarray(res.results[2 * b]["out"], dtype=np.float32)
        hi = np.asarray(res.results[2 * b + 1]["out"], dtype=np.float32)[::-1]
        outs.append(np.concatenate([lo, hi], 0))
    return np.stack(outs, 0)
```
